# Optimizing a Trainium2 kernel written in Bass

```python
import math
import jax, jax.numpy as jnp
from jax import lax
import numpy as np

D_MODEL = 2048
BATCH = 2
SEQ = 4096
DEPTH = 1

D_MIX = D_MODEL
D_ATTN = D_MIX // 2
D_CONV = D_MIX - D_ATTN
N_HEADS = 8
DV = D_ATTN // N_HEADS
DQ = DV // 2
ROT_DIM = DQ // 4
ROPE_THETA = 500000.0
CONV_WIDTH = 31
CONV_GROUPS = 8
Q_BLOCK = 128
EPS = 1e-6
D_IN = 3 * D_ATTN + D_ATTN + 2 * D_CONV + D_CONV

kernel_name = "hymba_diffattn_conformer_conv_layer"


def rmsnorm(x, g):
    xf = x.astype(jnp.float32)
    y = xf * lax.rsqrt(jnp.mean(xf * xf, axis=-1, keepdims=True) + EPS)
    return (y * g.astype(jnp.float32)).astype(x.dtype)


def layernorm(x, g, b):
    xf = x.astype(jnp.float32)
    mu = jnp.mean(xf, axis=-1, keepdims=True)
    var = jnp.mean(jnp.square(xf - mu), axis=-1, keepdims=True)
    y = (xf - mu) * lax.rsqrt(var + EPS)
    return (y * g.astype(jnp.float32) + b.astype(jnp.float32)).astype(x.dtype)


def rope_tables(positions):
    inv_freq = ROPE_THETA ** (-jnp.arange(0, ROT_DIM, 2, dtype=jnp.float32) / ROT_DIM)
    ang = positions.astype(jnp.float32)[..., None] * inv_freq
    return jnp.cos(ang), jnp.sin(ang)


def apply_partial_rope(t, cos, sin):
    cos = cos[:, :, None, None, :].astype(t.dtype)
    sin = sin[:, :, None, None, :].astype(t.dtype)
    t_rot, t_pass = t[..., :ROT_DIM], t[..., ROT_DIM:]
    t1, t2 = t_rot[..., : ROT_DIM // 2], t_rot[..., ROT_DIM // 2:]
    rotated = jnp.concatenate([t1 * cos - t2 * sin, t2 * cos + t1 * sin], axis=-1)
    return jnp.concatenate([rotated, t_pass], axis=-1)


def diff_attention(q, k, v, lam):
    B, H, S = q.shape[0], q.shape[1], q.shape[2]
    scale = 1.0 / math.sqrt(DQ)
    kpos = jnp.arange(S)

    def one_block(i):
        start = i * Q_BLOCK
        qb = lax.dynamic_slice_in_dim(q, start, Q_BLOCK, axis=2)
        s = jnp.einsum('bhqcd,bhkcd->bhcqk', qb, k).astype(jnp.float32) * scale
        qpos = start + jnp.arange(Q_BLOCK)
        causal = kpos[None, :] <= qpos[:, None]
        s = jnp.where(causal, s, -1e30)
        p = jax.nn.softmax(s, axis=-1)
        a = p[:, :, 0] - lam * p[:, :, 1]
        return jnp.einsum('bhqk,bhkd->bhqd', a.astype(v.dtype), v)

    outs = lax.map(one_block, jnp.arange(S // Q_BLOCK))
    outs = jnp.transpose(outs, (1, 0, 3, 2, 4))
    return outs.reshape(B, S, H, DV)


def causal_depthwise_conv(y, w, b):
    out = lax.conv_general_dilated(
        y, w[:, None, :].astype(y.dtype), window_strides=(1,),
        padding=[(CONV_WIDTH - 1, 0)],
        dimension_numbers=('NWC', 'WIO', 'NWC'),
        feature_group_count=y.shape[-1])
    return out + b.astype(y.dtype)


def setup_inputs(seed: int = 0) -> dict:
    key = jax.random.key(seed)
    ks = jax.random.split(key, 24)
    f32 = jnp.float32
    nrm = lambda k, shape, s: jax.random.normal(k, shape, f32) * s
    return {
        "x": nrm(ks[0], (BATCH, SEQ, D_MODEL), 1.0),
        "c": nrm(ks[1], (BATCH, D_MODEL), 1.0),
        "positions": jnp.broadcast_to(jnp.arange(SEQ, dtype=jnp.int32), (BATCH, SEQ)),
        "norm_g": 1.0 + nrm(ks[2], (DEPTH, D_MODEL), 0.01),
        "w_ada": nrm(ks[3], (DEPTH, D_MODEL, 3 * D_MODEL), D_MODEL ** -0.5),
        "b_ada": nrm(ks[4], (DEPTH, 3 * D_MODEL), 0.01),
        "w_in": nrm(ks[5], (DEPTH, D_MODEL, D_IN), D_MODEL ** -0.5),
        "lambda_q1": nrm(ks[6], (DEPTH, DQ), 0.1),
        "lambda_k1": nrm(ks[7], (DEPTH, DQ), 0.1),
        "lambda_q2": nrm(ks[8], (DEPTH, DQ), 0.1),
        "lambda_k2": nrm(ks[9], (DEPTH, DQ), 0.1),
        "subln_g": 1.0 + nrm(ks[10], (DEPTH, DV), 0.01),
        "conv_dw_w": nrm(ks[11], (DEPTH, CONV_WIDTH, D_CONV), CONV_WIDTH ** -0.5),
        "conv_dw_b": nrm(ks[12], (DEPTH, D_CONV), 0.01),
        "conv_ln_g": 1.0 + nrm(ks[13], (DEPTH, D_CONV), 0.01),
        "conv_ln_b": nrm(ks[14], (DEPTH, D_CONV), 0.01),
        "w_pw": nrm(ks[15], (DEPTH, D_CONV, D_CONV), D_CONV ** -0.5),
        "b_pw": nrm(ks[16], (DEPTH, D_CONV), 0.01),
        "w_out": nrm(ks[17], (DEPTH, D_MIX, D_MODEL), D_MIX ** -0.5),
        "final_g": 1.0 + nrm(ks[18], (D_MODEL,), 0.01),
    }


def reference(x, c, positions, norm_g, w_ada, b_ada, w_in, lambda_q1, lambda_k1,
              lambda_q2, lambda_k2, subln_g, conv_dw_w, conv_dw_b, conv_ln_g, conv_ln_b,
              w_pw, b_pw, w_out, final_g):
    B, S, _ = x.shape
    cos, sin = rope_tables(positions)
    c_act = jax.nn.silu(c)
    splits = [D_ATTN, 2 * D_ATTN, 3 * D_ATTN, 4 * D_ATTN, 4 * D_ATTN + 2 * D_CONV]

    for l in range(DEPTH):
        mod = c_act @ w_ada[l] + b_ada[l]
        shift, scale, gate = jnp.split(mod, 3, axis=-1)
        h = rmsnorm(x, norm_g[l]) * (1.0 + scale[:, None, :]) + shift[:, None, :]

        z = h @ w_in[l]
        q, k, v, g_attn, u, g_conv = jnp.split(z, splits, axis=-1)

        lam_init = 0.8 - 0.6 * math.exp(-0.3 * l)
        lam = (jnp.exp(jnp.sum(lambda_q1[l].astype(jnp.float32) * lambda_k1[l].astype(jnp.float32)))
               - jnp.exp(jnp.sum(lambda_q2[l].astype(jnp.float32) * lambda_k2[l].astype(jnp.float32)))
               + lam_init)
        q = apply_partial_rope(q.reshape(B, S, N_HEADS, 2, DQ), cos, sin)
        k = apply_partial_rope(k.reshape(B, S, N_HEADS, 2, DQ), cos, sin)
        q = jnp.transpose(q, (0, 2, 1, 3, 4))
        k = jnp.transpose(k, (0, 2, 1, 3, 4))
        vh = jnp.transpose(v.reshape(B, S, N_HEADS, DV), (0, 2, 1, 3))
        o = diff_attention(q, k, vh, lam)
        o = rmsnorm(o, subln_g[l]) * (1.0 - lam_init)
        y_attn = o.reshape(B, S, D_ATTN) * jax.nn.silu(g_attn)

        u_a, u_b = jnp.split(u, 2, axis=-1)
        y = u_a * jax.nn.sigmoid(u_b)
        y = causal_depthwise_conv(y, conv_dw_w[l], conv_dw_b[l])
        y = jax.nn.silu(layernorm(y, conv_ln_g[l], conv_ln_b[l]))
        y = y @ w_pw[l] + b_pw[l]
        y_conv = y * jax.nn.silu(g_conv)

        mixed = jnp.concatenate([y_attn, y_conv], axis=-1) @ w_out[l]
        x = x + gate[:, None, :] * mixed

    return rmsnorm(x, final_g)
```

```python
import math
import os
from contextlib import ExitStack

import numpy as np
import ml_dtypes

import concourse.bass as bass
import concourse.mybir as mybir
from concourse.bass_utils import run_bass_kernel_spmd

F32 = mybir.dt.float32
BF16 = mybir.dt.bfloat16
I32 = mybir.dt.int32
U8 = mybir.dt.uint8
AF = mybir.ActivationFunctionType
ALU = mybir.AluOpType
AX = mybir.AxisListType

D = 2048
S = 4096
NB = 32
T = 1024
NM = 8
EXT = 160
TE = NM * EXT
H = 8
DV = 128
DQ = 64
DIN = 7168
EPS = 1e-6
LAM_INIT = 0.8 - 0.6 * math.exp(-0.3 * 0)
CW = 31
THETA = 500000.0
TG = 256
KC = 16
VP = 136


class Sched:
    CE = ("tensor", "vector", "scalar", "gpsimd")
    QS = ("sync", "gpsimd", "scalar")

    def __init__(self, nc, stack, n_dma_sems=6):
        self.nc = nc
        self.ops = {e: [] for e in ("tensor", "vector", "scalar", "gpsimd", "sync")}
        self.cnt = {e: 0 for e in self.CE}
        self.sem = {e: stack.enter_context(nc.semaphore("s_" + e)) for e in self.CE}
        self.dsem = {q: [stack.enter_context(nc.semaphore("d_%s%d" % (q, i))) for i in range(n_dma_sems)]
                     for q in self.QS}
        self.dcnt = {q: [0] * n_dma_sems for q in self.QS}
        self.dnext = {q: 0 for q in self.QS}
        self.lastw = {}
        self.readers = {}
        self.waited = {e: {} for e in self.ops}
        self.pending = {e: False for e in self.CE}

    def _semobj(self, key):
        if isinstance(key, str):
            return self.sem[key]
        return self.dsem[key[1]][key[2]]

    def _deps(self, reads, writes):
        deps = []
        for r in reads:
            t = self.lastw.get(r)
            if t is not None:
                deps.append(t)
        for w in writes:
            t = self.lastw.get(w)
            if t is not None:
                deps.append(t)
            rd = self.readers.get(w)
            if rd:
                deps.extend(rd.items())
        return deps

    def _record(self, tok, reads, writes):
        for w in writes:
            self.lastw[w] = tok
            self.readers[w] = {}
        for r in reads:
            d = self.readers.setdefault(r, {})
            if d.get(tok[0], 0) < tok[1]:
                d[tok[0]] = tok[1]

    def _filter(self, eng, deps):
        best = {}
        for key, val in deps:
            if key == eng and (eng == "tensor" or val > self.cnt[eng]):
                continue
            if best.get(key, 0) < val:
                best[key] = val
        out = []
        w = self.waited[eng]
        for key, val in best.items():
            if w.get(key, 0) >= val:
                continue
            w[key] = val
            out.append((self._semobj(key), val))
        return out

    def op(self, eng, fn, reads=(), writes=(), sig=True):
        deps = self._deps(reads, writes)
        waits = self._filter(eng, deps)
        if sig:
            self.cnt[eng] += 1
            tok = (eng, self.cnt[eng])
            self.ops[eng].append((fn, waits, self.sem[eng], 1))
            self.pending[eng] = False
        else:
            tok = (eng, self.cnt[eng] + 1)
            self.ops[eng].append((fn, waits, None, 0))
            self.pending[eng] = True
        self._record(tok, reads, writes)
        return tok

    def dma(self, q, fn, reads=(), writes=()):
        deps = self._deps(reads, writes)
        i = self.dnext[q]
        self.dnext[q] = (i + 1) % len(self.dsem[q])
        key = ("d", q, i)
        if self.dcnt[q][i] > 0:
            deps.append((key, self.dcnt[q][i]))
        self.dcnt[q][i] += 16
        tok = (key, self.dcnt[q][i])
        waits = self._filter(q, deps)
        self.ops[q].append((fn, waits, self.dsem[q][i], 16))
        self._record(tok, reads, writes)
        return tok

    def fence(self, engines=("tensor", "vector", "scalar", "gpsimd", "sync")):
        for e in self.CE:
            assert not self.pending[e], e
        toks = [(e, self.cnt[e]) for e in self.CE if self.cnt[e] > 0]
        for q in self.QS:
            for i, c in enumerate(self.dcnt[q]):
                if c > 0:
                    toks.append((("d", q, i), c))
        for e in engines:
            waits = self._filter(e, toks)
            if waits:
                self.ops[e].append((None, waits, None, 0))
        self.lastw.clear()
        self.readers.clear()

    def emit(self, name, e):
        for fn, waits, sem, inc in self.ops[name]:
            for s, v in waits:
                e.wait_ge(s, v)
            if fn is None:
                continue
            ins = fn(e)
            if sem is not None:
                ins.then_inc(sem, inc)


class Arena:
    def __init__(self, nc, nbytes):
        self.t = nc.alloc_sbuf_tensor("arena", [128, nbytes], U8)
        self.nbytes = nbytes

    def buf(self, off, shape, dt):
        esz = {F32: 4, BF16: 2, I32: 4}[dt]
        n = 1
        for s in shape:
            n *= s
        assert off % 32 == 0, off
        assert off + n * esz <= self.nbytes, (off, n * esz, self.nbytes)
        a = self.t[:, off:off + n * esz].bitcast(dt)
        if len(shape) == 2:
            a = a.rearrange("p (a b) -> p a b", a=shape[0])
        elif len(shape) == 3:
            a = a.rearrange("p (a b c) -> p a b c", a=shape[0], b=shape[1])
        return a


def KB(x):
    return int(x * 1024)


def build_program(debug=None, stop=None):
    debug = debug or ()
    nc = bass.Bass("TRN2", target_bir_lowering=False)
    stack = ExitStack()
    stack.enter_context(nc.allow_low_precision("bf16 matmul operands, fp32 accumulation"))
    stack.enter_context(nc.allow_non_contiguous_dma("small strided loads"))

    def din(name, shape, dt=F32):
        return nc.dram_tensor(name, list(shape), dt, kind="ExternalInput").ap()

    xT = din("xT", [(S // TG) * 128, KC * TG])
    xoT = din("xoT", [(TE // TG) * 128, KC * TG])
    xo = din("xo", [T, D])
    c_col = din("c_col", [128, KC])
    pos_full = din("pos_full", [128, NB], I32)
    pos_own = din("pos_own", [128, NM], I32)
    invf = din("invf", [128, 8])
    normg_col = din("normg_col", [128, KC])
    final_g_b = din("final_g_b", [128, D])
    subln_g_b = din("subln_g_b", [128, DV])
    dww_col = din("dww_col", [128, 8 * CW])
    dwb_col = din("dwb_col", [128, 8])
    lng_col = din("lng_col", [128, 8])
    lnb_col = din("lnb_col", [128, 8])
    bpw_col = din("bpw_col", [128, 8])
    b_ada = din("b_ada", [1, 3 * D])
    lam_vecs = din("lam_vecs", [128, 4 * DQ])
    w_ada = din("w_ada", [8 * 128, KC * 512])
    w_gate = din("w_gate", [16 * 128, KC * 128])
    w_in = din("w_in", [D, DIN])
    w_pw = din("w_pw", [1024, 1024])
    w_out = din("w_out", [D, D])
    masks_in = din("masks", [128, 4 * 128], BF16)
    ident_in = din("ident", [128, 128], BF16)
    ymask_in = din("ymask", [128, 32])
    out = nc.dram_tensor("out", [T, D], F32, kind="ExternalOutput").ap()

    KT_d = nc.dram_tensor("KT_d", [H, 128, S], BF16).ap()
    V_d = nc.dram_tensor("V_d", [NB, 128, H * VP], BF16).ap()

    dbg = {}
    for name, shape, dt in debug:
        dbg[name] = nc.dram_tensor("dbg_" + name, list(shape), dt, kind="ExternalOutput").ap()

    A = Arena(nc, KB(200))
    psum = nc.alloc_psum_tensor("ps", [128, 4096], F32)

    def bank(i, n=512):
        return psum[:, i * 512:i * 512 + n]

    def bank_bf(i):
        return psum[:, i * 512:(i + 1) * 512].bitcast(BF16)

    Sd = Sched(nc, stack)
    OP = Sd.op
    DMA = Sd.dma

    o = 0

    def P(shape, dt):
        nonlocal o
        esz = 2 if dt == BF16 else 4
        n = 1
        for s_ in shape:
            n *= s_
        b = A.buf(o, shape, dt)
        o += ((n * esz + 31) // 32) * 32
        return b

    ident = P([128], BF16)
    ones_bf = P([128], BF16)
    one_f = P([32], F32)
    ones_row = P([128], F32)
    masks = P([4, 128], BF16)
    invf_t = P([8], F32)
    ga_col = P([KC], F32)
    shift_col = P([KC], F32)
    normg_t = P([KC], F32)
    ccol_t = P([KC], F32)
    cact_t = P([KC], F32)
    dww_t = P([8, CW], F32)
    dwb_t = P([8], F32)
    lng_t = P([8], F32)
    lnb_t = P([8], F32)
    bpw_t = P([8], F32)
    subg_t = P([DV], F32)
    ymask_t = P([32], F32)
    lamv_t = P([4, DQ], F32)
    lam_tmp = P([2, DQ], F32)
    eps_t = P([8], F32)
    lam_s = P([8], F32)
    posf_i = P([NB], I32)
    poso_i = P([NM], I32)
    posf_f = P([NB], F32)
    poso_f = P([NM], F32)
    ang_f = P([NB, 8], F32)
    ang_o = P([NM, 8], F32)
    ang_i = P([NB, 8], I32)
    ang_k = P([NB, 8], F32)
    ang_n = P([NB, 8], F32)
    cos_f = P([NB, 8], F32)
    sin_f = P([NB, 8], F32)
    cos_o = P([NM, 8], F32)
    sin_o = P([NM, 8], F32)
    final_g_t = P([D], F32)
    gate_b = P([D], F32)
    assert o <= KB(30), o

    QT = A.buf(KB(30), [H, T], BF16)
    G = A.buf(KB(46), [NM, 1024], BF16)
    ycT = A.buf(KB(62), [8, T], BF16)
    ya = A.buf(KB(30), [NM, 1024], BF16)
    yaT = A.buf(KB(94), [8, T], BF16)

    def ld(dst, src, q="sync", r=(), w=()):
        return DMA(q, lambda e: e.dma_start(out=dst, in_=src), r, w)

    wada = [A.buf(KB(110), [KC, 512], F32), A.buf(KB(142), [KC, 512], F32)]
    brow = [A.buf(KB(174), [512], F32), A.buf(KB(176), [512], F32)]
    mrow = [A.buf(KB(178), [512], F32), A.buf(KB(180), [512], F32)]
    ld(ccol_t, c_col, w=["ccol"])
    for g_ in range(2):
        ld(wada[g_], w_ada[g_ * 128:(g_ + 1) * 128, :].rearrange("p (j n) -> p j n", j=KC), r=(), w=["wada%d" % g_])
        ld(brow[g_][0:1, :], b_ada[0:1, g_ * 512:(g_ + 1) * 512], w=["brow%d" % g_])
    OP("scalar", lambda e: e.activation(out=cact_t, in_=ccol_t, func=AF.Silu), ["ccol"], ["cact"])
    ld(ident, ident_in, w=["ident"])
    ld(masks, masks_in.rearrange("p (a b) -> p a b", a=4), w=["masks"])
    ld(invf_t, invf, w=["invf"])
    ld(normg_t, normg_col, w=["normg"])
    ld(dww_t, dww_col.rearrange("p (a b) -> p a b", a=8), w=["dww"])
    ld(dwb_t, dwb_col, w=["dwb"])
    ld(lng_t, lng_col, w=["lng"])
    ld(lnb_t, lnb_col, w=["lnb"])
    ld(bpw_t, bpw_col, w=["bpw"])
    ld(subg_t, subln_g_b, w=["subg"])
    ld(ymask_t, ymask_in, w=["ymask"])
    ld(lamv_t, lam_vecs.rearrange("p (a b) -> p a b", a=4), w=["lamv"])
    ld(posf_i, pos_full, w=["posfi"])
    ld(poso_i, pos_own, w=["posoi"])
    ld(final_g_t, final_g_b, w=["fing"])

    OP("vector", lambda e: e.memset(ones_bf, 1.0), (), ["ones_bf"])
    OP("vector", lambda e: e.memset(one_f, 1.0), (), ["one_f"])
    OP("vector", lambda e: e.memset(ones_row, 1.0), (), ["ones_row"])

    OP("vector", lambda e: e.memset(eps_t, EPS), (), ["eps_t"])

    def rsqrt(dst, src, scale, reads, wtag):
        OP("scalar", lambda e: e.activation(out=dst, in_=src, func=AF.Sqrt, bias=eps_t[:, 0:1], scale=float(scale)),
           list(reads) + ["eps_t"], [wtag])
        OP("vector", lambda e: e.reciprocal(out=dst, in_=dst), [wtag], [wtag])

    def lam_block():
        OP("vector", lambda e: e.tensor_tensor(out=lam_tmp[:, 0, :], in0=lamv_t[:, 0, :], in1=lamv_t[:, 1, :], op=ALU.mult),
           ["lamv"], ["lamtmp0"])
        OP("vector", lambda e: e.tensor_tensor(out=lam_tmp[:, 1, :], in0=lamv_t[:, 2, :], in1=lamv_t[:, 3, :], op=ALU.mult),
           ["lamv"], ["lamtmp1"])
        OP("vector", lambda e: e.reduce_sum(out=lam_s[:, 0:2], in_=lam_tmp, axis=AX.X), ["lamtmp0", "lamtmp1"], ["lams01"])
        OP("scalar", lambda e: e.activation(out=lam_s[:, 4:6], in_=lam_s[:, 0:2], func=AF.Exp), ["lams01"], ["lams45"])
        OP("vector", lambda e: e.tensor_tensor(out=lam_s[:, 2:3], in0=lam_s[:, 4:5], in1=lam_s[:, 5:6], op=ALU.subtract),
           ["lams45"], ["lams2"])
        OP("vector", lambda e: e.tensor_scalar(out=lam_s[:, 3:4], in0=lam_s[:, 2:3], scalar1=float(LAM_INIT), scalar2=-1.0,
                                                op0=ALU.add, op1=ALU.mult), ["lams2"], ["neglam"])

    TWO_PI = 2.0 * math.pi

    def rope_tables(pos_i, pos_f, ang, cos_t, sin_t, nb, tag):
        OP("vector", lambda e: e.tensor_copy(out=pos_f, in_=pos_i), ["pos%si" % tag], ["pos%sf" % tag])
        OP("vector", lambda e: e.tensor_tensor(out=ang, in0=pos_f.unsqueeze(2).to_broadcast([128, nb, 8]),
                                                in1=invf_t.unsqueeze(1).to_broadcast([128, nb, 8]), op=ALU.mult),
           ["pos%sf" % tag, "invf"], ["ang" + tag])
        ki = ang_i[:, 0:nb, :]
        kf = ang_k[:, 0:nb, :]
        ng = ang_n[:, 0:nb, :]
        for dst, off, nm in ((sin_t, 0.5, "sin"), (cos_t, 0.75, "cos")):
            tg_ = nm + tag
            OP("vector", (lambda dst=dst, off=off: lambda e: e.tensor_scalar(out=dst, in0=ang, scalar1=1.0 / TWO_PI, scalar2=off,
                                                    op0=ALU.mult, op1=ALU.add))(), ["ang" + tag], [tg_])
            OP("vector", (lambda dst=dst: lambda e: e.tensor_copy(out=ki, in_=dst))(), [tg_], ["ang_i"])
            OP("vector", lambda e: e.tensor_copy(out=kf, in_=ki), ["ang_i"], ["ang_k"])
            OP("vector", (lambda dst=dst: lambda e: e.tensor_tensor(out=dst, in0=dst, in1=kf, op=ALU.subtract))(),
               [tg_, "ang_k"], [tg_])
            OP("vector", (lambda dst=dst: lambda e: e.tensor_scalar(out=ng, in0=dst, scalar1=0.0, scalar2=None, op0=ALU.is_lt))(),
               [tg_], ["ang_n"])
            OP("vector", (lambda dst=dst: lambda e: e.tensor_tensor(out=dst, in0=dst, in1=ng, op=ALU.add))(),
               [tg_, "ang_n"], [tg_])
            OP("vector", (lambda dst=dst: lambda e: e.tensor_scalar(out=dst, in0=dst, scalar1=TWO_PI, scalar2=-math.pi,
                                                    op0=ALU.mult, op1=ALU.add))(), [tg_], [tg_])
            OP("vector", (lambda dst=dst: lambda e: e.tensor_scalar(out=dst, in0=dst, scalar1=3.1415925, scalar2=-3.1415925,
                                                    op0=ALU.min, op1=ALU.max))(), [tg_], [tg_])
            OP("scalar", (lambda dst=dst: lambda e: e.activation(out=dst, in_=dst, func=AF.Sin))(), [tg_], [tg_])

    def const_block():
        lam_block()
        rope_tables(posf_i, posf_f, ang_f, cos_f, sin_f, NB, "f")
        rope_tables(poso_i, poso_f, ang_o, cos_o, sin_o, NM, "o")

    if stop == "c":
        const_block()

    if stop == "c":
        if "cs" in dbg:
            ld(dbg["cs"][:, 0:256], cos_f.rearrange("p a b -> p (a b)"), r=["cosf"])
            ld(dbg["cs"][:, 256:512], sin_f.rearrange("p a b -> p (a b)"), r=["sinf"])
            ld(dbg["mod"][:, 32:33], lam_s[:, 3:4], r=["neglam"])
        return nc, stack, Sd, dbg
    Wkv = A.buf(KB(30), [KC, 2048], BF16)
    hT = [A.buf(KB(152), [KC, TG], BF16), A.buf(KB(160), [KC, TG], BF16)]
    ktok = [A.buf(KB(168), [1024], BF16), A.buf(KB(170), [1024], BF16)]
    KTst = [A.buf(KB(172), [H, TG], BF16), A.buf(KB(176), [H, TG], BF16)]
    Vst = [A.buf(KB(180), [2, H * VP], BF16), A.buf(KB(185), [2, H * VP], BF16)]
    rt_p = [A.buf(KB(190) + i * 512, [16, 8], F32) for i in range(4)]
    w_in_v = w_in.rearrange("(j p) n -> p j n", p=128)

    FL0 = os.environ.get("P_FLAGS", "")
    for t4 in range(4) if "nowkv" not in FL0 else ():
        cs = slice(1024 + t4 * 512, 1024 + (t4 + 1) * 512)
        DMA("gpsimd", (lambda t4=t4, cs=cs: lambda e: e.dma_start(out=Wkv[:, :, t4 * 512:(t4 + 1) * 512],
                                                                   in_=w_in_v[:, :, cs]))(), (), ["Wkv"])

    def mod_group(g):
        s = g % 2
        cols = slice(g * 512, (g + 1) * 512)
        if g >= 2:
            ld(wada[s], w_ada[g * 128:(g + 1) * 128, :].rearrange("p (j n) -> p j n", j=KC), r=(), w=["wada%d" % s])
            ld(brow[s][0:1, :], b_ada[0:1, cols], w=["brow%d" % s])
        for j in range(KC):
            OP("tensor", (lambda j=j: lambda e: e.matmul(bank(0)[0:1, :], cact_t[:, j:j + 1], wada[s][:, j, :],
                                                          start=(j == 0), stop=(j == KC - 1)))(),
               ["cact", "wada%d" % s], ["psM"], sig=(j == KC - 1))
        OP("vector", lambda e: e.tensor_tensor(out=mrow[s][0:1, :], in0=bank(0)[0:1, :], in1=brow[s][0:1, :], op=ALU.add),
           ["psM", "brow%d" % s], ["mrow%d" % s])
        if g < 8:
            for i in range(4):
                OP("tensor", (lambda i=i: lambda e: e.matmul(bank(1)[:, g * 4 + i:g * 4 + i + 1],
                                                              mrow[s][0:1, i * 128:(i + 1) * 128], one_f[0:1, 0:1],
                                                              start=True, stop=True))(),
                   ["mrow%d" % s, "one_f"], ["pscol"], sig=(i == 3))
        else:
            OP("tensor", lambda e: e.matmul(bank(2), ones_row[0:1, :], mrow[s][0:1, :], start=True, stop=True),
               ["mrow%d" % s, "ones_row"], ["psG"])
            gc = slice((g - 8) * 512, (g - 7) * 512)
            OP("scalar", lambda e: e.activation(out=gate_b[:, gc], in_=bank(2), func=AF.Copy), ["psG"], ["gate_b"])

    for g in range(8):
        mod_group(g)
        if g == 1:
            const_block()
    OP("vector", lambda e: e.tensor_copy(out=shift_col, in_=bank(1)[:, 0:16]), ["pscol"], ["shift_col"])
    OP("vector", lambda e: e.scalar_tensor_tensor(out=ga_col, in0=bank(1)[:, 16:32], scalar=1.0, in1=normg_t,
                                                   op0=ALU.add, op1=ALU.mult), ["pscol", "normg"], ["ga_col"])
    if "mod" in dbg:
        ld(dbg["mod"][:, 0:16], shift_col, r=["shift_col"])
        ld(dbg["mod"][:, 16:32], ga_col, r=["ga_col"])
        ld(dbg["mod"][:, 32:33], lam_s[:, 3:4], r=["neglam"])
        ld(dbg["gate"], gate_b, r=["gate_b"])
        ld(dbg["cs"][:, 0:256], cos_f.rearrange("p a b -> p (a b)"), r=["cosf"])
        ld(dbg["cs"][:, 256:512], sin_f.rearrange("p a b -> p (a b)"), r=["sinf"])
    Sd.fence()
    if stop == "m":
        return nc, stack, Sd, dbg

    xs = [A.buf(KB(110), [KC, TG], F32), A.buf(KB(126), [KC, TG], F32)]
    sq = A.buf(KB(142), [KC, TG], BF16)
    rstd = [A.buf(KB(150), [TG], F32), A.buf(KB(151), [TG], F32)]

    def h_front(src, s, tag):
        h_front_a(src, s)
        h_front_b(s)

    def h_front_a(src, s):
        h_ld(src, s)
        h_sq(s)

    def h_ld(src, s):
        ld(xs[s], src, w=["xs%d" % s])

    def h_sq(s):
        OP("scalar", lambda e: e.activation(out=sq, in_=xs[s], func=AF.Square), ["xs%d" % s], ["sq"])

    def h_front_b(s):
        for j in range(KC):
            OP("tensor", (lambda j=j: lambda e: e.matmul(bank(0)[:, 0:TG], ones_bf, sq[:, j, :],
                                                          start=(j == 0), stop=(j == KC - 1)))(),
               ["sq", "ones_bf"], ["ps_ss"], sig=(j == KC - 1))

    def h_back(s, dst, dst_tag, split=False):
        h_back_1(s)
        h_back_2(s, dst, dst_tag, split)

    def h_back_1(s):
        rsqrt(rstd[s], bank(0)[:, 0:TG], 1.0 / D, ["ps_ss"], "rstd%d" % s)
        OP("vector", lambda e: e.tensor_tensor(out=xs[s], in0=xs[s], in1=rstd[s].unsqueeze(1).to_broadcast([128, KC, TG]),
                                                op=ALU.mult), ["xs%d" % s, "rstd%d" % s], ["xs%d" % s])

    def h_back_2(s, dst, dst_tag, split=False):
        act_js = list(range(0, KC, 2)) if split else list(range(KC))
        for j in act_js:
            OP("scalar", (lambda j=j: lambda e: e.activation(out=dst[:, j, :], in_=xs[s][:, j, :], func=AF.Identity,
                                                              bias=shift_col[:, j:j + 1], scale=ga_col[:, j:j + 1]))(),
               ["xs%d" % s, "ga_col", "shift_col"], [dst_tag], sig=(j == act_js[-1]))
        for j in (range(1, KC, 2) if split else ()):
            OP("vector", (lambda j=j: lambda e: e.tensor_scalar(out=dst[:, j, :], in0=xs[s][:, j, :],
                                                                 scalar1=ga_col[:, j:j + 1], scalar2=shift_col[:, j:j + 1],
                                                                 op0=ALU.mult, op1=ALU.add))(),
               ["xs%d" % s, "ga_col", "shift_col"], [dst_tag], sig=(j == KC - 1))

    for s in range(2) if "novmem" not in FL0 else ():
        OP("vector", (lambda s=s: lambda e: e.memset(Vst[s], 1.0))(), (), ["Vst%d" % s])

    def xT_grp(g):
        return xT[g * 128:(g + 1) * 128, :].rearrange("p (j t) -> p j t", j=KC)

    NG = int(os.environ.get('P_NG', S // TG))

    def rope(ps_k, nh, cos_t, sin_t, dst, src_tag, dst_tag, rt=None):
        rt = rt or rt_p
        kv = ps_k.rearrange("p (h c d) -> p (h c) d", h=nh, c=2)
        dv = dst.rearrange("p (h c d) -> p (h c) d", h=nh, c=2)
        n2 = nh * 2
        cb = cos_t.unsqueeze(1).to_broadcast([128, n2, 8])
        sb = sin_t.unsqueeze(1).to_broadcast([128, n2, 8])
        t1, t2 = kv[:, :, 0:8], kv[:, :, 8:16]
        r0, r1, r2, r3 = [r[:, 0:n2, :] for r in rt]
        OP("vector", lambda e: e.tensor_tensor(out=r0, in0=t1, in1=cb, op=ALU.mult), [src_tag], ["rt0"])
        OP("vector", lambda e: e.tensor_tensor(out=r1, in0=t2, in1=sb, op=ALU.mult), [src_tag], ["rt1"])
        OP("vector", lambda e: e.tensor_tensor(out=r2, in0=t2, in1=cb, op=ALU.mult), [src_tag], ["rt2"])
        OP("vector", lambda e: e.tensor_tensor(out=r3, in0=t1, in1=sb, op=ALU.mult), [src_tag], ["rt3"])
        OP("vector", lambda e: e.tensor_tensor(out=dv[:, :, 0:8], in0=r0, in1=r1, op=ALU.subtract),
           ["rt0", "rt1"], [dst_tag])
        OP("vector", lambda e: e.tensor_tensor(out=dv[:, :, 8:16], in0=r2, in1=r3, op=ALU.add),
           ["rt2", "rt3"], [dst_tag])
        OP("scalar", lambda e: e.activation(out=dv[:, :, 16:64], in_=kv[:, :, 16:64], func=AF.Copy),
           [src_tag], [dst_tag])

    FL = os.environ.get("P_FLAGS", "")

    def p_kv(tg, blk):
        s = tg % 2
        gb = tg * 2 + blk
        if "nokv" in FL:
            return
        for t4 in range(4):
            for j in range(KC):
                OP("tensor", (lambda t4=t4, j=j: lambda e: e.matmul(bank(1 + t4), hT[s][:, j, blk * 128:(blk + 1) * 128],
                                                                     Wkv[:, j, t4 * 512:(t4 + 1) * 512],
                                                                     start=(j == 0), stop=(j == KC - 1)))(),
                   ["hT%d" % s, "Wkv"], ["ps_k%d" % t4 if t4 < 2 else "ps_kv%d" % t4], sig=(j == KC - 1))
        for t4 in (2, 3) if "novev" not in FL else ():
            hh = (t4 - 2) * 4
            dstv = Vst[s][:, blk, hh * VP:(hh + 4) * VP].rearrange("p (h d) -> p h d", h=4)[:, :, 0:128]
            OP("scalar", (lambda t4=t4, dstv=dstv: lambda e: e.activation(
                out=dstv, in_=bank(t4 + 1).rearrange("p (h d) -> p h d", h=4), func=AF.Copy))(),
               ["ps_kv%d" % t4], ["Vst%d" % s])
        kb = gb % 2
        for hb in range(2) if "norope" not in FL else ():
            rope(bank(1 + hb), 4, cos_f[:, gb, :], sin_f[:, gb, :], ktok[kb][:, hb * 512:(hb + 1) * 512],
                 "ps_k%d" % hb, "ktok%d" % kb)

    def p_tr(tg, blk):
        s = tg % 2
        if "notr" in FL or "nokv" in FL:
            return
        gb = tg * 2 + blk
        kb = gb % 2
        for h in range(H):
            OP("tensor", (lambda h=h: lambda e: e.transpose(bank_bf(5)[:, h * 128:(h + 1) * 128],
                                                             ktok[kb][:, h * 128:(h + 1) * 128], ident))(),
               ["ktok%d" % kb, "ident"], ["ps_tr"], sig=(h == H - 1))
        OP("vector", lambda e: e.tensor_copy(out=KTst[s][:, :, blk * 128:(blk + 1) * 128],
                                              in_=bank_bf(5).rearrange("p (h t) -> p h t", h=H)),
           ["ps_tr"], ["KTst%d" % s])
        if blk == 1 and "nost" not in FL:
            ld(KT_d[:, :, tg * TG:(tg + 1) * TG].rearrange("h p t -> p h t"), KTst[s], q="gpsimd", r=["KTst%d" % s], w=["KT_d"])
            ld(V_d[tg * 2:tg * 2 + 2].rearrange("b p n -> p b n"), Vst[s], q="gpsimd", r=["Vst%d" % s], w=["V_d"])

    wg = [A.buf(KB(94), [KC, 128], F32), A.buf(KB(102), [KC, 128], F32)]
    gacc = [A.buf(KB(192), [128], F32), A.buf(KB(192.5), [128], F32)]
    bg = [A.buf(KB(193), [128], F32), A.buf(KB(193.5), [128], F32)]

    def gate_load(i):
        s_ = i % 2
        ld(wg[s_], w_gate[i * 128:(i + 1) * 128, :].rearrange("p (j n) -> p j n", j=KC), w=["wg%d" % s_])
        ld(bg[s_][0:1, :], b_ada[0:1, 2 * D + i * 128:2 * D + (i + 1) * 128], w=["bg%d" % s_])

    def gate_tile(i):
        s_ = i % 2
        OP("vector", lambda e: e.tensor_scalar(out=gacc[s_], in0=wg[s_][:, 0, :], scalar1=cact_t[:, 0:1], scalar2=None,
                                                op0=ALU.mult), ["wg%d" % s_, "cact"], ["gacc%d" % s_])
        for j in range(1, KC):
            OP("vector", (lambda j=j: lambda e: e.scalar_tensor_tensor(out=gacc[s_], in0=wg[s_][:, j, :],
                                                                        scalar=cact_t[:, j:j + 1], in1=gacc[s_],
                                                                        op0=ALU.mult, op1=ALU.add))(),
               ["wg%d" % s_, "cact", "gacc%d" % s_], ["gacc%d" % s_])
        OP("vector", lambda e: e.tensor_tensor(out=gacc[s_][0:1, :], in0=gacc[s_][0:1, :],
                                                in1=bg[s_][0:1, :], op=ALU.add),
           ["gacc%d" % s_, "bg%d" % s_], ["gacc%d" % s_])

    def gate_reduce(i):
        s_ = i % 2
        OP("tensor", lambda e: e.matmul(bank(6)[:, 0:128], ones_row, gacc[s_], start=True, stop=True),
           ["gacc%d" % s_, "ones_row"], ["ps_gate"])
        OP("scalar", lambda e: e.activation(out=gate_b[:, i * 128:(i + 1) * 128], in_=bank(6)[:, 0:128], func=AF.Copy),
           ["ps_gate"], ["gate_b"])

    h_front(xT_grp(0), 0, "f")
    h_back(0, hT[0], "hT0")
    pending_tr = None
    for tg in range(NG):
        nxt = tg + 1 < NG
        if nxt:
            h_front_a(xT_grp(tg + 1), (tg + 1) % 2)
        p_kv(tg, 0)
        if nxt:
            h_front_b((tg + 1) % 2)
        if pending_tr is not None:
            p_tr(*pending_tr)
        pending_tr = (tg, 0)
        if nxt:
            h_back((tg + 1) % 2, hT[(tg + 1) % 2], "hT%d" % ((tg + 1) % 2))
        p_kv(tg, 1)
        p_tr(*pending_tr)
        pending_tr = (tg, 1)
        if tg == 0:
            gate_load(0)
        if tg + 1 < NG:
            gate_load(tg + 1)
        if tg > 0:
            gate_reduce(tg - 1)
        gate_tile(tg)
    p_tr(*pending_tr)
    gate_reduce(NG - 1)
    if "KT" in dbg:
        Sd.fence()
        ld(dbg["KT"], KT_d, r=["KT_d"])
        ld(dbg["V"], V_d, r=["V_d"])
    Sd.fence()
    if stop == "p":
        return nc, stack, Sd, dbg

    hTo = A.buf(KB(160), [KC, TE], BF16)
    def xoT_grp(g):
        return xoT[g * 128:(g + 1) * 128, :].rearrange("p (j t) -> p j t", j=KC)

    NGO = TE // TG
    Wpre = [A.buf(KB(62), [KC, 512], BF16), A.buf(KB(78), [KC, 512], BF16)]
    for i_ in range(2):
        DMA("gpsimd", (lambda i_=i_: lambda e: e.dma_start(out=Wpre[i_], in_=w_in_v[:, :, i_ * 512:(i_ + 1) * 512]))(),
            (), ["Wpre%d" % i_])
    xs.append(A.buf(KB(94), [KC, TG], F32))
    rstd.append(A.buf(KB(152), [TG], F32))
    for g in range(min(3, NGO)):
        h_ld(xoT_grp(g), g % 3)
    h_sq(0)
    h_front_b(0)
    h_back_1(0)
    for g in range(NGO):
        if g + 1 < NGO:
            h_sq((g + 1) % 3)
            h_front_b((g + 1) % 3)
            h_back_1((g + 1) % 3)
        h_back_2(g % 3, hTo[:, :, g * TG:(g + 1) * TG], "hTo", split=True)
        if g + 3 < NGO:
            h_ld(xoT_grp(g + 3), g % 3)
    Sd.fence()
    if stop == "oh":
        return nc, stack, Sd, dbg

    Wst = [A.buf(KB(98 + 16 * i), [KC, 512], BF16) for i in range(3)]
    qtok = [A.buf(KB(146), [512], BF16), A.buf(KB(147), [512], BF16)]
    stmp = [A.buf(KB(148), [512], F32), A.buf(KB(150), [512], F32)]
    rt_o = [A.buf(KB(152) + i * 512, [16, 8], F32) for i in range(4)]
    yext = A.buf(KB(78), [8, TE], BF16)

    wstate = {"n": 0}

    def load_w(col0):
        slot = wstate["n"] % 3
        wstate["n"] += 1
        DMA("gpsimd", lambda e: e.dma_start(out=Wst[slot], in_=w_in_v[:, :, col0:col0 + 512]), (), ["Wst%d" % slot])
        return slot

    tile_cols = [3072, 3584, 5120, 4096, 5632, 4608, 6144, 6656]
    slots = {}

    def want(i):
        if i < len(tile_cols) and i not in slots:
            slots[i] = load_w(tile_cols[i])

    def own(j, m):
        return hTo[:, j, m * EXT + 32:m * EXT + 160]

    def q_tr(cg, m):
        for i in range(4):
            OP("tensor", (lambda i=i: lambda e: e.transpose(bank_bf(5)[:, i * 128:(i + 1) * 128],
                                                             qtok[m % 2][:, i * 128:(i + 1) * 128], ident))(),
               ["qtok%d" % (m % 2), "ident"], ["bk5"], sig=(i == 3))
        OP("vector", lambda e: e.tensor_copy(out=QT[:, cg * 4:(cg + 1) * 4, m * 128:(m + 1) * 128],
                                              in_=bank_bf(5)[:, 0:512].rearrange("p (h t) -> p h t", h=4)),
           ["bk5"], ["QT"])

    want(0)
    want(1)
    for cg in range(2):
        pend = None
        for m in range(NM):
            pb = 1 + (m % 2)
            for j in range(KC):
                OP("tensor", (lambda j=j, m=m, pb=pb, cg=cg: lambda e: e.matmul(bank(pb), own(j, m),
                                                                          Wpre[cg][:, j, :], start=(j == 0), stop=(j == KC - 1)))(),
                   ["hTo", "Wpre%d" % cg], ["bk%d" % pb], sig=(j == KC - 1))
            rope(bank(pb), 4, cos_o[:, m, :], sin_o[:, m, :], qtok[m % 2], "bk%d" % pb, "qtok%d" % (m % 2), rt=rt_o)
            if pend is not None:
                q_tr(*pend)
            pend = (cg, m)
        q_tr(*pend)
    for cg in range(2):
        ti = cg
        want(ti + 2)
        sl = slots[ti]
        for m in range(NM):
            pb = 3 + (m % 2)
            for j in range(KC):
                OP("tensor", (lambda j=j, m=m, pb=pb, sl=sl: lambda e: e.matmul(bank(pb), own(j, m),
                                                                          Wst[sl][:, j, :], start=(j == 0), stop=(j == KC - 1)))(),
                   ["hTo", "Wst%d" % sl], ["bk%d" % pb], sig=(j == KC - 1))
            OP("scalar", (lambda m=m, pb=pb: lambda e: e.activation(out=stmp[m % 2], in_=bank(pb), func=AF.Silu))(),
               ["bk%d" % pb], ["stmp%d" % (m % 2)])
            OP("vector", (lambda m=m, cg=cg: lambda e: e.scalar_tensor_tensor(
                out=G[:, m, cg * 512:(cg + 1) * 512].rearrange("p (h d) -> p h d", h=4),
                in0=stmp[m % 2].rearrange("p (h d) -> p h d", h=4), scalar=float(1.0 - LAM_INIT),
                in1=subg_t.unsqueeze(1).to_broadcast([128, 4, 128]), op0=ALU.mult, op1=ALU.mult))(),
               ["stmp%d" % (m % 2), "subg"], ["G"])
    if "QT" in dbg:
        ld(dbg["QT"], QT.rearrange("p h t -> p (h t)"), r=["QT"])
        ld(dbg["G"], G.rearrange("p m f -> p (m f)"), r=["G"])

    pieces = [(0, 512), (512, 512), (1024, 256)]
    kctr = 0
    for t in range(2):
        for which in range(2):
            ti = 2 + 2 * t + which
            want(ti + 2)
            sl = slots[ti]
            for c4 in range(4):
                cc = 4 * t + c4
                for (p0, n) in pieces:
                    bk = 1 + (kctr % 4)
                    for j in range(KC):
                        OP("tensor", (lambda j=j, sl=sl, bk=bk, p0=p0, n=n, c4=c4: lambda e: e.matmul(
                            bank(bk)[:, 0:n], Wst[sl][:, j, c4 * 128:(c4 + 1) * 128], hTo[:, j, p0:p0 + n],
                            start=(j == 0), stop=(j == KC - 1)))(),
                           ["hTo", "Wst%d" % sl], ["bk%d" % bk], sig=(j == KC - 1))
                    if which == 0:
                        OP("scalar", (lambda bk=bk, n=n, p0=p0, cc=cc: lambda e: e.activation(
                            out=yext[:, cc, p0:p0 + n], in_=bank(bk)[:, 0:n], func=AF.Sigmoid))(),
                           ["bk%d" % bk], ["yext%d" % cc])
                    else:
                        OP("vector", (lambda bk=bk, n=n, p0=p0, cc=cc: lambda e: e.tensor_tensor(
                            out=yext[:, cc, p0:p0 + n], in0=bank(bk)[:, 0:n], in1=yext[:, cc, p0:p0 + n], op=ALU.mult))(),
                           ["bk%d" % bk, "yext%d" % cc], ["yext%d" % cc])
                    kctr += 1
    OP("vector", lambda e: e.tensor_tensor(out=yext[:, :, 0:32], in0=yext[:, :, 0:32],
                                            in1=ymask_t.unsqueeze(1).to_broadcast([128, 8, 32]), op=ALU.mult),
       ["yext%d" % cc_ for cc_ in range(8)] + ["ymask"], ["yext%d" % cc_ for cc_ in range(8)])
    for t in range(2):
        ti = 6 + t
        want(ti + 2)
        sl = slots[ti]
        for c4 in range(4):
            cc = 4 * t + c4
            for hf in range(2):
                bk = 5 + hf
                for j in range(KC):
                    OP("tensor", (lambda j=j, sl=sl, c4=c4, hf=hf, bk=bk: lambda e: e.matmul(
                        bank(bk), Wst[sl][:, j, c4 * 128:(c4 + 1) * 128],
                        hTo[:, j, :].rearrange("p (m t) -> p m t", t=EXT)[:, 4 * hf:4 * hf + 4, 32:160],
                        start=(j == 0), stop=(j == KC - 1)))(),
                       ["hTo", "Wst%d" % sl], ["bk%d" % bk], sig=(j == KC - 1))
                OP("scalar", (lambda cc=cc, hf=hf, bk=bk: lambda e: e.activation(
                    out=ycT[:, cc, hf * 512:(hf + 1) * 512], in_=bank(bk), func=AF.Silu))(),
                   ["bk%d" % bk], ["ycT"])
    Sd.fence()
    if stop == "cp":
        return nc, stack, Sd, dbg

    sw = A.buf(KB(98), [8, 512], BF16)
    vf = A.buf(KB(106), [8, T], F32)
    vb = A.buf(KB(138), [8, 512], BF16)
    vsq = A.buf(KB(146), [8, 512], BF16)
    st = [A.buf(KB(154) + i * 2048, [512], F32) for i in range(4)]
    diag = [A.buf(KB(162), [CW, 128], BF16), A.buf(KB(170), [CW, 128], BF16)]
    Wpw = A.buf(KB(178), [8, 1024], BF16)
    DMA("gpsimd", lambda e: e.dma_start(out=Wpw, in_=w_pw.rearrange("(j p) n -> p j n", p=128)), (), ["Wpw"])
    yext_v = [yext[:, cc, :].rearrange("p (m t) -> p m t", t=EXT) for cc in range(8)]
    dctr = {"n": 0, "b": 0}

    def diag_build(cc):
        k = dctr["b"]
        dctr["b"] += 1
        dg = diag[k % 2]
        dtag = "diag%d" % (k % 2)
        for j in range(CW):
            OP("vector", (lambda j=j, cc=cc, dg=dg: lambda e: e.tensor_scalar(
                out=dg[:, j, :], in0=ident, scalar1=dww_t[:, cc, j:j + 1], scalar2=None, op0=ALU.mult))(),
               ["ident", "dww"], [dtag], sig=(j == CW - 1))

    def conv_cc(cc, hf):
        dg = diag[dctr["n"] % 2]
        dtag = "diag%d" % (dctr["n"] % 2)
        dctr["n"] += 1
        bk = 1 + (dctr["n"] % 2)
        for j in range(CW):
            OP("tensor", (lambda j=j, cc=cc, hf=hf, bk=bk, dg=dg: lambda e: e.matmul(
                bank(bk), dg[:, j, :], yext_v[cc][:, 4 * hf:4 * hf + 4, 2 + j:2 + j + 128],
                start=(j == 0), stop=(j == CW - 1)))(),
               [dtag, "yext%d" % cc], ["ps_cv%d" % bk], sig=(j == CW - 1))
        OP("scalar", (lambda cc=cc, hf=hf, bk=bk: lambda e: e.activation(
            out=vf[:, cc, hf * 512:(hf + 1) * 512], in_=bank(bk), func=AF.Identity, bias=dwb_t[:, cc:cc + 1], scale=1.0))(),
           ["ps_cv%d" % bk, "dwb"], ["vf%d" % hf])

    def chain_front(hf):
        vh_ = vf[:, :, hf * 512:(hf + 1) * 512]
        OP("vector", (lambda vh_=vh_: lambda e: e.tensor_copy(out=vb, in_=vh_))(), ["vf%d" % hf], ["vb"])
        OP("scalar", (lambda vh_=vh_: lambda e: e.activation(out=vsq, in_=vh_, func=AF.Square))(), ["vf%d" % hf], ["vsq"])

    def chain_stats(hf):
        for cc in range(8):
            OP("tensor", (lambda cc=cc: lambda e: e.matmul(bank(3), ones_bf, vb[:, cc, :], start=(cc == 0), stop=(cc == 7)))(),
               ["vb", "ones_bf"], ["ps_s1"], sig=(cc == 7))
        for cc in range(8):
            OP("tensor", (lambda cc=cc: lambda e: e.matmul(bank(4), ones_bf, vsq[:, cc, :], start=(cc == 0), stop=(cc == 7)))(),
               ["vsq", "ones_bf"], ["ps_s2"], sig=(cc == 7))

    def chain_mid(hf):
        vh_ = vf[:, :, hf * 512:(hf + 1) * 512]
        OP("vector", lambda e: e.tensor_scalar(out=st[0], in0=bank(3), scalar1=1.0 / 1024, scalar2=None, op0=ALU.mult),
           ["ps_s1"], ["st0"])
        OP("vector", lambda e: e.tensor_tensor(out=st[1], in0=st[0], in1=st[0], op=ALU.mult), ["st0"], ["st1"])
        OP("vector", lambda e: e.scalar_tensor_tensor(out=st[2], in0=bank(4), scalar=1.0 / 1024, in1=st[1],
                                                       op0=ALU.mult, op1=ALU.subtract), ["ps_s2", "st1"], ["st2"])
        rsqrt(st[3], st[2], 1.0, ["st2"], "st3")
        OP("vector", (lambda vh_=vh_: lambda e: e.tensor_tensor(out=vh_, in0=vh_, in1=st[0].unsqueeze(1).to_broadcast([128, 8, 512]),
                                                                op=ALU.subtract))(), ["vf%d" % hf, "st0"], ["vf%d" % hf])
        OP("vector", (lambda vh_=vh_: lambda e: e.tensor_tensor(out=vh_, in0=vh_, in1=st[3].unsqueeze(1).to_broadcast([128, 8, 512]),
                                                                op=ALU.mult))(), ["vf%d" % hf, "st3"], ["vf%d" % hf])

    def chain_silu(hf):
        for cc in range(8):
            OP("scalar", (lambda cc=cc, hf=hf: lambda e: e.activation(
                out=sw[:, cc, :], in_=vf[:, cc, hf * 512:(hf + 1) * 512], func=AF.Silu,
                bias=lnb_t[:, cc:cc + 1], scale=lng_t[:, cc:cc + 1]))(),
               ["vf%d" % hf, "lng", "lnb"], ["sw"])

    def pw(hf):
        for co in range(8):
            bk = 5 + (co % 2)
            for ci in range(8):
                OP("tensor", (lambda co=co, ci=ci, bk=bk: lambda e: e.matmul(
                    bank(bk), Wpw[:, ci, co * 128:(co + 1) * 128], sw[:, ci, :], start=(ci == 0), stop=(ci == 7)))(),
                   ["Wpw", "sw"], ["ps_pw%d" % bk], sig=(ci == 7))
            OP("vector", (lambda co=co, hf=hf, bk=bk: lambda e: e.scalar_tensor_tensor(
                out=ycT[:, co, hf * 512:(hf + 1) * 512], in0=bank(bk), scalar=bpw_t[:, co:co + 1],
                in1=ycT[:, co, hf * 512:(hf + 1) * 512], op0=ALU.add, op1=ALU.mult))(),
               ["ps_pw%d" % bk, "bpw", "ycT"], ["ycT"])

    seq = [(cc, 0) for cc in range(8)] + [(cc, 1) for cc in range(8)]
    diag_build(seq[0][0])
    for i, (cc, hf) in enumerate(seq):
        conv_cc(cc, hf)
        if i + 1 < len(seq):
            diag_build(seq[i + 1][0])
        if hf == 1 and cc == 0:
            chain_front(0)
        if hf == 1 and cc == 1:
            chain_stats(0)
        if hf == 1 and cc == 2:
            chain_mid(0)
        if hf == 1 and cc == 4:
            chain_silu(0)
    chain_front(1)
    chain_stats(1)
    pw(0)
    chain_mid(1)
    chain_silu(1)
    kt0_pre = A.buf(KB(110), [S], BF16)
    vh0_pre = A.buf(KB(126), [NB, VP], BF16)
    ld(kt0_pre, KT_d[0], r=(), w=["kt0", "vf0", "vf1"])
    for q4 in range(4):
        ld(vh0_pre[:, q4 * 8:(q4 + 1) * 8, :],
           V_d[q4 * 8:(q4 + 1) * 8, :, 0:VP].rearrange("b p n -> p b n"), r=(), w=["vh0", "vf0", "vf1"])
    pw(1)
    Sd.fence()
    if "ycT" in dbg:
        ld(dbg["ycT"], ycT.rearrange("p c t -> p (c t)"), r=["ycT"])
        Sd.fence()
    if stop == "o":
        return nc, stack, Sd, dbg

    kt = [A.buf(KB(110), [S], BF16), A.buf(KB(118), [S], BF16)]
    vh = [A.buf(KB(126), [NB, VP], BF16), A.buf(KB(135), [NB, VP], BF16)]
    NSL = 3
    pT = [A.buf(KB(144 + 2 * i), [1024], BF16) for i in range(NSL)]
    dtmp = [A.buf(KB(150), [128], F32), A.buf(KB(150.5), [128], F32)]
    rec = [A.buf(KB(151), [8], F32), A.buf(KB(151) + 32, [8], F32)]
    ssq = A.buf(KB(151) + 64, [8], F32)
    rsv = A.buf(KB(151) + 96, [8], F32)
    dd_h = A.buf(KB(152), [NM, 128], F32)
    sqd = A.buf(KB(156), [NM, 128], F32)
    SC_BANK = [0, 2, 6]

    def acc_ap(am, c):
        return psum[:, (4 + am) * 512 + c * 256:(4 + am) * 512 + c * 256 + 130]

    NPRE = 10
    late_off = {10: 110, 11: 114, 12: 126, 13: 130, 14: 118, 15: 122}
    late_tag = {10: "kt0", 11: "kt0", 12: "vh0", 13: "vh0", 14: "kt1", 15: "kt1"}
    Wout = [A.buf(KB(160) + k * 4096, [D], BF16) if k < NPRE else A.buf(KB(late_off[k]), [D], BF16)
            for k in range(KC)]
    w_out_v = w_out.rearrange("(j p) n -> p j n", p=128)
    for k in range(NPRE):
        DMA("gpsimd", (lambda k=k: lambda e: e.dma_start(out=Wout[k], in_=w_out_v[:, k, :]))(), (), ["Wout%d" % k])

    QQ = A.buf(KB(78), [H, NM * 256], BF16)
    QQv = QQ.rearrange("p h (m c q) -> p (h m) c q", c=2, q=128)
    QTv = QT.rearrange("p h (m q) -> p (h m) q", q=128)
    OP("vector", lambda e: e.memset(QQv[0:64, :, 1, :], 0.0), (), ["QQa"])
    OP("scalar", lambda e: e.activation(out=QQv[64:128, :, 0, :], in_=QTv[64:128], func=AF.Copy, scale=0.0), ["QT"], ["QQb"])
    for h in range(H):
        for c in range(2):
            dst = QQ[c * 64:(c + 1) * 64, h, :].rearrange("p (m c q) -> p m c q", c=2, q=128)[:, :, c, :]
            src = QT[c * 64:(c + 1) * 64, h, :].rearrange("p (m q) -> p m q", q=128)
            if c == 0:
                OP("vector", (lambda dst=dst, src=src: lambda e: e.tensor_copy(out=dst, in_=src))(), ["QT", "QQa"], ["QQa"])
            else:
                OP("scalar", (lambda dst=dst, src=src: lambda e: e.activation(out=dst, in_=src, func=AF.Copy))(), ["QT", "QQb"], ["QQb"])
    state = {"pc": 0, "q": []}

    zeros_bf = A.buf(KB(151) + 128, [128], BF16)
    OP("vector", lambda e: e.memset(zeros_bf, 0.0), (), ["zeros_bf"])

    def flush_one():
        pi = state["q"].pop(0)
        if pi[2] == 0:
            am_, s__ = pi[6], pi[5]
            OP("tensor", lambda e: e.matmul(psum[:, (4 + am_) * 512:(4 + am_) * 512 + 386], zeros_bf, kt[s__][:, 0:386],
                                            start=True, stop=False),
               ["zeros_bf", "kt%d" % s__], ["ps_acc%d" % am_], sig=False)
        av(pi)
        if pi[2] == pi[3] - 1:
            post(pi[0], pi[1], pi[6])
            if pi[1] == NM - 1:
                epilogue_a(pi[0])
                state["epi"] = pi[0]

    def av(item):
        (h, m, t, nq, sl, s_, am) = item
        for e_ in range(4):
            blk = 4 * t + e_
            for c in range(2):
                OP("tensor", (lambda e_=e_, c=c, blk=blk, sl=sl, s_=s_, am=am, t=t, nq=nq: lambda e: e.matmul(
                    acc_ap(am, c), pT[sl][:, (e_ * 2 + c) * 128:(e_ * 2 + c + 1) * 128], vh[s_][:, blk, 0:130],
                    start=False, stop=(t == nq - 1 and e_ == 3 and c == 1)))(),
                   ["pT%d" % sl, "vh%d" % s_], ["ps_acc%d" % am], sig=(e_ == 3 and c == 1))

    def post(h, m, am):
        rc = rec[am]
        a0 = acc_ap(am, 0)
        a1 = acc_ap(am, 1)
        sums = psum[:, (4 + am) * 512:(5 + am) * 512].rearrange("p (c n) -> p c n", c=2)[:, :, 128]
        OP("vector", lambda e: e.reciprocal(out=rc[:, 0:2], in_=sums), ["ps_acc%d" % am], ["rec%d" % am])
        OP("vector", lambda e: e.tensor_tensor(out=rc[:, 2:3], in0=rc[:, 1:2], in1=lam_s[:, 3:4], op=ALU.mult),
           ["rec%d" % am, "neglam"], ["rec%d" % am])
        OP("vector", lambda e: e.tensor_scalar(out=dtmp[am], in0=a0[:, 0:128], scalar1=rc[:, 0:1], scalar2=None, op0=ALU.mult),
           ["ps_acc%d" % am, "rec%d" % am], ["dtmp%d" % am])
        OP("vector", lambda e: e.scalar_tensor_tensor(out=dd_h[:, m, :], in0=a1[:, 0:128], scalar=rc[:, 2:3], in1=dtmp[am],
                                                       op0=ALU.mult, op1=ALU.add),
           ["ps_acc%d" % am, "rec%d" % am, "dtmp%d" % am], ["dd_h"])

    def epilogue_a(h):
        OP("vector", lambda e: e.tensor_tensor(out=sqd, in0=dd_h, in1=dd_h, op=ALU.mult), ["dd_h"], ["sqd"])
        OP("vector", lambda e: e.reduce_sum(out=ssq[:, 0:8], in_=sqd, axis=AX.X), ["sqd"], ["ssq"])
        OP("vector", lambda e: e.tensor_copy(out=sqd, in_=dd_h), ["dd_h", "ssq"], ["sqd"])

    def epilogue_b(h):
        OP("scalar", lambda e: e.activation(out=rsv[:, 0:8], in_=ssq[:, 0:8], func=AF.Ln, bias=eps_t[:, 0:1], scale=1.0 / DV),
           ["ssq", "eps_t"], ["rsv"])
        OP("scalar", lambda e: e.activation(out=rsv[:, 0:8], in_=rsv[:, 0:8], func=AF.Exp, scale=-0.5), ["rsv"], ["rsv"])
        OP("vector", lambda e: e.tensor_tensor(out=sqd, in0=sqd, in1=rsv[:, 0:8].unsqueeze(2).to_broadcast([128, NM, 128]),
                                                op=ALU.mult), ["sqd", "rsv"], ["sqd"])
        OP("vector", lambda e: e.tensor_tensor(out=ya[:, :, h * 128:(h + 1) * 128], in0=sqd,
                                                in1=G[:, :, h * 128:(h + 1) * 128], op=ALU.mult), ["sqd", "G"], ["ya"])

    hm = 0
    for h in range(H):
        s_ = h % 2
        if h > 0:
            ld(kt[s_], KT_d[h], r=["KT_d"], w=["kt%d" % s_])
            for q4 in range(4):
                ld(vh[s_][:, q4 * 8:(q4 + 1) * 8, :],
                   V_d[q4 * 8:(q4 + 1) * 8, :, h * VP:(h + 1) * VP].rearrange("b p n -> p b n"), r=["V_d"], w=["vh%d" % s_])
        for m in range(NM):
            am = hm % 2
            nq = m + 1
            for t in range(nq):
                sl = state["pc"] % NSL
                state["pc"] += 1
                scp = psum[:, SC_BANK[sl] * 512:SC_BANK[sl] * 512 + 1024]
                for e_ in range(4):
                    blk = 4 * t + e_
                    OP("tensor", (lambda e_=e_, blk=blk, scp=scp, s_=s_, h=h, m=m: lambda e: e.matmul(
                        scp[:, e_ * 256:(e_ + 1) * 256],
                        kt[s_][:, blk * 128:(blk + 1) * 128],
                        QQ[:, h, m * 256:(m + 1) * 256], start=True, stop=True))(),
                       ["kt%d" % s_, "QQa", "QQb"], ["ps_sc%d" % sl], sig=(e_ == 3))
                OP("scalar", (lambda sl=sl, scp=scp: lambda e: e.activation(out=pT[sl], in_=scp, func=AF.Exp,
                                                                            scale=1.0 / math.sqrt(DQ)))(),
                   ["ps_sc%d" % sl], ["pT%d" % sl])
                if t == nq - 1:
                    OP("vector", (lambda sl=sl: lambda e: e.tensor_tensor(
                        out=pT[sl].rearrange("p (a c q) -> p a c q", a=4, c=2),
                        in0=pT[sl].rearrange("p (a c q) -> p a c q", a=4, c=2),
                        in1=masks.unsqueeze(2).to_broadcast([128, 4, 2, 128]), op=ALU.mult))(),
                       ["pT%d" % sl, "masks"], ["pT%d" % sl])
                state["q"].append((h, m, t, nq, sl, s_, am))
                if len(state["q"]) > NSL - 1:
                    flush_one()
            if h == H - 1 and m == 2:
                for k in (10, 11, 12, 13):
                    DMA("gpsimd", (lambda k=k: lambda e: e.dma_start(out=Wout[k], in_=w_out_v[:, k, :]))(), (),
                        ["Wout%d" % k, late_tag[k]])
            if state.get("epi") is not None and m == 2:
                epilogue_b(state["epi"])
                state["epi"] = None
            hm += 1
    while state["q"]:
        flush_one()
    if state.get("epi") is not None:
        epilogue_b(state["epi"])
    for m in range(NM):
        for h in range(H):
            OP("tensor", (lambda h=h, m=m: lambda e: e.transpose(bank_bf(0)[:, h * 128:(h + 1) * 128],
                                                                  ya[:, m, h * 128:(h + 1) * 128], ident))(),
               ["ya", "ident"], ["ps_tr", "ps_sc0"], sig=(h == H - 1))
        OP("vector", (lambda m=m: lambda e: e.tensor_copy(out=yaT[:, :, m * 128:(m + 1) * 128],
                                                           in_=bank_bf(0).rearrange("p (h t) -> p h t", h=H)))(),
           ["ps_tr"], ["yaT"])
    Sd.fence()
    if "yaT" in dbg:
        ld(dbg["yaT"], yaT.rearrange("p c t -> p (c t)"), r=["yaT"])
        Sd.fence()
    if stop == "a":
        return nc, stack, Sd, dbg

    for k in (14, 15):
        DMA("gpsimd", (lambda k=k: lambda e: e.dma_start(out=Wout[k], in_=w_out_v[:, k, :]))(), (), ["Wout%d" % k])
    xo_t = [A.buf(KB(78), [D], F32), A.buf(KB(86), [D], F32)]
    xnew = [A.buf(KB(134), [D], F32), A.buf(KB(142), [D], F32)]
    junkb = A.buf(KB(150), [D], BF16)
    fs = A.buf(KB(154), [8], F32)
    for m in range(NM):
        p_ = m % 2
        ld(xo_t[p_], xo[m * 128:(m + 1) * 128, :], w=["xo%d" % p_])
        for n4 in range(4):
            bk = 4 * p_ + n4
            cols = slice(n4 * 512, (n4 + 1) * 512)
            for k in range(KC):
                lhs = yaT[:, k, m * 128:(m + 1) * 128] if k < 8 else ycT[:, k - 8, m * 128:(m + 1) * 128]
                OP("tensor", (lambda k=k, lhs=lhs, bk=bk, cols=cols: lambda e: e.matmul(
                    bank(bk), lhs, Wout[k][:, cols], start=(k == 0), stop=(k == KC - 1)))(),
                   ["yaT", "ycT", "Wout%d" % k], ["ps_o%d" % bk], sig=(k == KC - 1))
            OP("vector", (lambda bk=bk, cols=cols, p_=p_: lambda e: e.tensor_tensor(
                out=xnew[p_][:, cols], in0=bank(bk), in1=gate_b[:, cols], op=ALU.mult))(),
               ["ps_o%d" % bk, "gate_b"], ["xnew%d" % p_])
            OP("vector", (lambda cols=cols, p_=p_: lambda e: e.tensor_tensor(
                out=xnew[p_][:, cols], in0=xnew[p_][:, cols], in1=xo_t[p_][:, cols], op=ALU.add))(),
               ["xnew%d" % p_, "xo%d" % p_], ["xnew%d" % p_])
        OP("vector", lambda e: e.memset(fs[:, 0:1], 0.0), (), ["fs"])
        OP("scalar", (lambda p_=p_: lambda e: e.activation(out=junkb, in_=xnew[p_], func=AF.Square, accum_out=fs[:, 0:1]))(),
           ["xnew%d" % p_, "fs"], ["fs", "junkb"])
        rsqrt(fs[:, 1:2], fs[:, 0:1], 1.0 / D, ["fs"], "fs1")
        OP("vector", (lambda p_=p_: lambda e: e.scalar_tensor_tensor(
            out=xnew[p_], in0=xnew[p_], scalar=fs[:, 1:2], in1=final_g_t, op0=ALU.mult, op1=ALU.mult))(),
           ["xnew%d" % p_, "fs1", "fing"], ["xnew%d" % p_])
        ld(out[m * 128:(m + 1) * 128, :], xnew[p_], q="gpsimd", r=["xnew%d" % p_], w=["out"])
    return nc, stack, Sd, dbg


def finish_program(nc, stack, Sd):
    Sd.fence(engines=("sync",))
    with nc.Block() as block:
        @block.tensor
        def _(e):
            Sd.emit("tensor", e)

        @block.vector
        def _(e):
            Sd.emit("vector", e)

        @block.scalar
        def _(e):
            Sd.emit("scalar", e)

        @block.gpsimd
        def _(e):
            Sd.emit("gpsimd", e)

        @block.sync
        def _(e):
            Sd.emit("sync", e)
    stack.close()
    return nc


def col_layout(v, nchunk):
    return np.ascontiguousarray(np.asarray(v, np.float32).reshape(nchunk, 128).T)


def make_in_maps(x, c, positions, norm_g, w_ada, b_ada, w_in, lambda_q1, lambda_k1, lambda_q2, lambda_k2,
                 subln_g, conv_dw_w, conv_dw_b, conv_ln_g, conv_ln_b, w_pw, b_pw, w_out, final_g):
    f32 = np.float32
    x = np.asarray(x, f32)
    c = np.asarray(c, f32)
    positions = np.asarray(positions, np.int32)
    w_ada_f = np.asarray(w_ada, f32)[0]
    w_ada0 = np.ascontiguousarray(
        w_ada_f[:, :2 * D].reshape(KC, 128, 8, 512).transpose(2, 1, 0, 3).reshape(8 * 128, KC * 512))
    w_gate0 = np.ascontiguousarray(
        w_ada_f[:, 2 * D:].reshape(KC, 128, 16, 128).transpose(2, 1, 0, 3).reshape(16 * 128, KC * 128))
    w_in0 = np.ascontiguousarray(np.asarray(w_in, f32)[0])
    w_pw0 = np.ascontiguousarray(np.asarray(w_pw, f32)[0])
    w_out0 = np.ascontiguousarray(np.asarray(w_out, f32)[0])
    b_ada0 = np.ascontiguousarray(np.asarray(b_ada, f32)[0].reshape(1, -1))
    invf = (THETA ** (-np.arange(0, 16, 2, dtype=np.float32) / 16.0)).astype(f32)
    invf_b = np.ascontiguousarray(np.broadcast_to(invf[None, :], (128, 8)))
    ident = np.eye(128, dtype=f32).astype(ml_dtypes.bfloat16)
    lam_vecs = np.concatenate([np.asarray(v, f32)[0] for v in (lambda_q1, lambda_k1, lambda_q2, lambda_k2)])
    lam_b = np.ascontiguousarray(np.broadcast_to(lam_vecs[None, :], (128, 4 * DQ)))
    dww = np.asarray(conv_dw_w, f32)[0]
    dww_col = np.ascontiguousarray(dww.T.reshape(8, 128, CW).transpose(1, 0, 2).reshape(128, 8 * CW))
    common = dict(
        invf=invf_b, normg_col=col_layout(np.asarray(norm_g, f32)[0], KC),
        final_g_b=np.ascontiguousarray(np.broadcast_to(np.asarray(final_g, f32)[None, :], (128, D))),
        subln_g_b=np.ascontiguousarray(np.broadcast_to(np.asarray(subln_g, f32)[0][None, :], (128, DV))),
        dww_col=dww_col, dwb_col=col_layout(np.asarray(conv_dw_b, f32)[0], 8),
        lng_col=col_layout(np.asarray(conv_ln_g, f32)[0], 8), lnb_col=col_layout(np.asarray(conv_ln_b, f32)[0], 8),
        bpw_col=col_layout(np.asarray(b_pw, f32)[0], 8), b_ada=b_ada0, lam_vecs=lam_b,
        w_ada=w_ada0, w_gate=w_gate0, w_in=w_in0, w_pw=w_pw0, w_out=w_out0, ident=ident,
    )
    def grp_layout(xt):
        ntok = xt.shape[1]
        return np.ascontiguousarray(
            xt.reshape(KC, 128, ntok // TG, TG).transpose(2, 1, 0, 3).reshape((ntok // TG) * 128, KC * TG))

    xT_b = [grp_layout(x[b].T) for b in range(2)]
    in_maps = []
    own_idx = []
    for core in range(8):
        b, r = core // 4, core % 4
        blocks = [4 * m + r for m in range(NM)]
        idx = np.concatenate([np.arange(128 * i, 128 * i + 128) for i in blocks])
        own_idx.append((b, idx))
        ext = np.concatenate([np.arange(128 * i - 32, 128 * i + 128) for i in blocks])
        valid = ext >= 0
        xe = np.zeros((TE, D), f32)
        xe[valid] = x[b][ext[valid]]
        ymask = np.ones((128, 32), f32)
        if r == 0:
            ymask[:] = 0.0
        mk = np.zeros((128, 4, 128), f32)
        kv = np.arange(128)[:, None]
        q = np.arange(128)[None, :]
        for j in range(4):
            if j < r:
                mk[:, j, :] = 1.0
            elif j == r:
                mk[:, j, :] = (q >= kv).astype(f32)
        m = dict(common)
        m.update(
            xT=xT_b[b], xoT=grp_layout(xe.T), xo=np.ascontiguousarray(x[b][idx]),
            c_col=col_layout(c[b], KC),
            pos_full=np.ascontiguousarray(positions[b].reshape(NB, 128).T),
            pos_own=np.ascontiguousarray(positions[b][idx].reshape(NM, 128).T),
            masks=mk.reshape(128, 512).astype(ml_dtypes.bfloat16), ymask=ymask,
        )
        in_maps.append(m)
    return in_maps, own_idx


def kernel(**inputs):
    in_maps, own_idx = make_in_maps(**inputs)
    nc, stack, Sd, _ = build_program()
    finish_program(nc, stack, Sd)
    res = run_bass_kernel_spmd(nc, in_maps, core_ids=list(range(8)))
    outp = np.zeros((2, S, D), np.float32)
    for core in range(8):
        b, idx = own_idx[core]
        outp[b, idx] = np.asarray(res.results[core]["out"], np.float32)
    return outp
```

```python
import math
import os
from contextlib import ExitStack

import numpy as np
import ml_dtypes

import concourse.bass as bass
import concourse.mybir as mybir
from concourse.bass_utils import run_bass_kernel_spmd

F32 = mybir.dt.float32
BF16 = mybir.dt.bfloat16
I32 = mybir.dt.int32
U8 = mybir.dt.uint8
AF = mybir.ActivationFunctionType
ALU = mybir.AluOpType
AX = mybir.AxisListType

D = 2048
S = 4096
NB = 32
T = 1024
NM = 8
EXT = 160
TE = NM * EXT
H = 8
DV = 128
DQ = 64
DIN = 7168
EPS = 1e-6
LAM_INIT = 0.8 - 0.6 * math.exp(-0.3 * 0)
CW = 31
THETA = 500000.0
TG = 256
KC = 16
VP = 136


class Sched:
    CE = ("tensor", "vector", "scalar", "gpsimd")
    QS = ("sync", "gpsimd", "scalar")

    def __init__(self, nc, stack, n_dma_sems=6):
        self.nc = nc
        self.ops = {e: [] for e in ("tensor", "vector", "scalar", "gpsimd", "sync")}
        self.cnt = {e: 0 for e in self.CE}
        self.sem = {e: stack.enter_context(nc.semaphore("s_" + e)) for e in self.CE}
        self.dsem = {q: [stack.enter_context(nc.semaphore("d_%s%d" % (q, i))) for i in range(n_dma_sems)]
                     for q in self.QS}
        self.dcnt = {q: [0] * n_dma_sems for q in self.QS}
        self.dnext = {q: 0 for q in self.QS}
        self.lastw = {}
        self.readers = {}
        self.waited = {e: {} for e in self.ops}
        self.pending = {e: False for e in self.CE}

    def _semobj(self, key):
        if isinstance(key, str):
            return self.sem[key]
        return self.dsem[key[1]][key[2]]

    def _deps(self, reads, writes):
        deps = []
        for r in reads:
            t = self.lastw.get(r)
            if t is not None:
                deps.append(t)
        for w in writes:
            t = self.lastw.get(w)
            if t is not None:
                deps.append(t)
            rd = self.readers.get(w)
            if rd:
                deps.extend(rd.items())
        return deps

    def _record(self, tok, reads, writes):
        for w in writes:
            self.lastw[w] = tok
            self.readers[w] = {}
        for r in reads:
            d = self.readers.setdefault(r, {})
            if d.get(tok[0], 0) < tok[1]:
                d[tok[0]] = tok[1]

    def _filter(self, eng, deps):
        best = {}
        for key, val in deps:
            if key == eng and (eng == "tensor" or val > self.cnt[eng]):
                continue
            if best.get(key, 0) < val:
                best[key] = val
        out = []
        w = self.waited[eng]
        for key, val in best.items():
            if w.get(key, 0) >= val:
                continue
            w[key] = val
            out.append((self._semobj(key), val))
        return out

    def op(self, eng, fn, reads=(), writes=(), sig=True):
        deps = self._deps(reads, writes)
        waits = self._filter(eng, deps)
        if sig:
            self.cnt[eng] += 1
            tok = (eng, self.cnt[eng])
            self.ops[eng].append((fn, waits, self.sem[eng], 1))
            self.pending[eng] = False
        else:
            tok = (eng, self.cnt[eng] + 1)
            self.ops[eng].append((fn, waits, None, 0))
            self.pending[eng] = True
        self._record(tok, reads, writes)
        return tok

    def dma(self, q, fn, reads=(), writes=()):
        deps = self._deps(reads, writes)
        i = self.dnext[q]
        self.dnext[q] = (i + 1) % len(self.dsem[q])
        key = ("d", q, i)
        if self.dcnt[q][i] > 0:
            deps.append((key, self.dcnt[q][i]))
        self.dcnt[q][i] += 16
        tok = (key, self.dcnt[q][i])
        waits = self._filter(q, deps)
        self.ops[q].append((fn, waits, self.dsem[q][i], 16))
        self._record(tok, reads, writes)
        return tok

    def fence(self, engines=("tensor", "vector", "scalar", "gpsimd", "sync")):
        for e in self.CE:
            assert not self.pending[e], e
        toks = [(e, self.cnt[e]) for e in self.CE if self.cnt[e] > 0]
        for q in self.QS:
            for i, c in enumerate(self.dcnt[q]):
                if c > 0:
                    toks.append((("d", q, i), c))
        for e in engines:
            waits = self._filter(e, toks)
            if waits:
                self.ops[e].append((None, waits, None, 0))
        self.lastw.clear()
        self.readers.clear()

    def emit(self, name, e):
        for fn, waits, sem, inc in self.ops[name]:
            for s, v in waits:
                e.wait_ge(s, v)
            if fn is None:
                continue
            ins = fn(e)
            if sem is not None:
                ins.then_inc(sem, inc)


class Arena:
    def __init__(self, nc, nbytes):
        self.t = nc.alloc_sbuf_tensor("arena", [128, nbytes], U8)
        self.nbytes = nbytes

    def buf(self, off, shape, dt):
        esz = {F32: 4, BF16: 2, I32: 4}[dt]
        n = 1
        for s in shape:
            n *= s
        assert off % 32 == 0, off
        assert off + n * esz <= self.nbytes, (off, n * esz, self.nbytes)
        a = self.t[:, off:off + n * esz].bitcast(dt)
        if len(shape) == 2:
            a = a.rearrange("p (a b) -> p a b", a=shape[0])
        elif len(shape) == 3:
            a = a.rearrange("p (a b c) -> p a b c", a=shape[0], b=shape[1])
        return a


def KB(x):
    return int(x * 1024)


def build_program(debug=None, stop=None):
    debug = debug or ()
    nc = bass.Bass("TRN2", target_bir_lowering=False)
    stack = ExitStack()
    stack.enter_context(nc.allow_low_precision("bf16 matmul operands, fp32 accumulation"))
    stack.enter_context(nc.allow_non_contiguous_dma("small strided loads"))

    def din(name, shape, dt=F32):
        return nc.dram_tensor(name, list(shape), dt, kind="ExternalInput").ap()

    xT = din("xT", [(S // TG) * 128, KC * TG])
    xoT = din("xoT", [(TE // TG) * 128, KC * TG])
    xo = din("xo", [T, D])
    c_col = din("c_col", [128, KC])
    pos_full = din("pos_full", [128, NB], I32)
    pos_own = din("pos_own", [128, NM], I32)
    invf = din("invf", [128, 8])
    normg_col = din("normg_col", [128, KC])
    final_g_b = din("final_g_b", [128, D])
    subln_g_b = din("subln_g_b", [128, DV])
    dww_col = din("dww_col", [128, 8 * CW])
    dwb_col = din("dwb_col", [128, 8])
    lng_col = din("lng_col", [128, 8])
    lnb_col = din("lnb_col", [128, 8])
    bpw_col = din("bpw_col", [128, 8])
    b_ada = din("b_ada", [1, 3 * D])
    lam_vecs = din("lam_vecs", [128, 4 * DQ])
    w_ada = din("w_ada", [8 * 128, KC * 512])
    w_gate = din("w_gate", [16 * 128, KC * 128])
    w_in = din("w_in", [D, DIN])
    w_pw = din("w_pw", [1024, 1024])
    w_out = din("w_out", [D, D])
    masks_in = din("masks", [128, 4 * 128], BF16)
    ident_in = din("ident", [128, 128], BF16)
    ymask_in = din("ymask", [128, 32])
    out = nc.dram_tensor("out", [T, D], F32, kind="ExternalOutput").ap()

    KT_d = nc.dram_tensor("KT_d", [H, 128, S], BF16).ap()
    V_d = nc.dram_tensor("V_d", [NB, 128, H * VP], BF16).ap()

    dbg = {}
    for name, shape, dt in debug:
        dbg[name] = nc.dram_tensor("dbg_" + name, list(shape), dt, kind="ExternalOutput").ap()

    A = Arena(nc, KB(200))
    psum = nc.alloc_psum_tensor("ps", [128, 4096], F32)

    def bank(i, n=512):
        return psum[:, i * 512:i * 512 + n]

    def bank_bf(i):
        return psum[:, i * 512:(i + 1) * 512].bitcast(BF16)

    Sd = Sched(nc, stack)
    OP = Sd.op
    DMA = Sd.dma

    o = 0

    def P(shape, dt):
        nonlocal o
        esz = 2 if dt == BF16 else 4
        n = 1
        for s_ in shape:
            n *= s_
        b = A.buf(o, shape, dt)
        o += ((n * esz + 31) // 32) * 32
        return b

    ident = P([128], BF16)
    ones_bf = P([128], BF16)
    one_f = P([32], F32)
    ones_row = P([128], F32)
    masks = P([4, 128], BF16)
    invf_t = P([8], F32)
    ga_col = P([KC], F32)
    shift_col = P([KC], F32)
    normg_t = P([KC], F32)
    ccol_t = P([KC], F32)
    cact_t = P([KC], F32)
    dww_t = P([8, CW], F32)
    dwb_t = P([8], F32)
    lng_t = P([8], F32)
    lnb_t = P([8], F32)
    bpw_t = P([8], F32)
    subg_t = P([DV], F32)
    ymask_t = P([32], F32)
    lamv_t = P([4, DQ], F32)
    lam_tmp = P([2, DQ], F32)
    eps_t = P([8], F32)
    lam_s = P([8], F32)
    posf_i = P([NB], I32)
    poso_i = P([NM], I32)
    posf_f = P([NB], F32)
    poso_f = P([NM], F32)
    ang_f = P([NB, 8], F32)
    ang_o = P([NM, 8], F32)
    ang_i = P([NB, 8], I32)
    ang_k = P([NB, 8], F32)
    ang_n = P([NB, 8], F32)
    cos_f = P([NB, 8], F32)
    sin_f = P([NB, 8], F32)
    cos_o = P([NM, 8], F32)
    sin_o = P([NM, 8], F32)
    final_g_t = P([D], F32)
    gate_b = P([D], F32)
    assert o <= KB(30), o

    QT = A.buf(KB(30), [H, T], BF16)
    G = A.buf(KB(46), [NM, 1024], BF16)
    ycT = A.buf(KB(62), [8, T], BF16)
    ya = A.buf(KB(30), [NM, 1024], BF16)
    yaT = A.buf(KB(94), [8, T], BF16)

    def ld(dst, src, q="sync", r=(), w=()):
        return DMA(q, lambda e: e.dma_start(out=dst, in_=src), r, w)

    wada = [A.buf(KB(110), [KC, 512], F32), A.buf(KB(142), [KC, 512], F32)]
    brow = [A.buf(KB(174), [512], F32), A.buf(KB(176), [512], F32)]
    mrow = [A.buf(KB(178), [512], F32), A.buf(KB(180), [512], F32)]
    ld(ccol_t, c_col, w=["ccol"])
    for g_ in range(2):
        ld(wada[g_], w_ada[g_ * 128:(g_ + 1) * 128, :].rearrange("p (j n) -> p j n", j=KC), r=(), w=["wada%d" % g_])
        ld(brow[g_][0:1, :], b_ada[0:1, g_ * 512:(g_ + 1) * 512], w=["brow%d" % g_])
    OP("scalar", lambda e: e.activation(out=cact_t, in_=ccol_t, func=AF.Silu), ["ccol"], ["cact"])
    ld(ident, ident_in, w=["ident"])
    ld(masks, masks_in.rearrange("p (a b) -> p a b", a=4), w=["masks"])
    ld(invf_t, invf, w=["invf"])
    ld(normg_t, normg_col, w=["normg"])
    ld(dww_t, dww_col.rearrange("p (a b) -> p a b", a=8), w=["dww"])
    ld(dwb_t, dwb_col, w=["dwb"])
    ld(lng_t, lng_col, w=["lng"])
    ld(lnb_t, lnb_col, w=["lnb"])
    ld(bpw_t, bpw_col, w=["bpw"])
    ld(subg_t, subln_g_b, w=["subg"])
    ld(ymask_t, ymask_in, w=["ymask"])
    ld(lamv_t, lam_vecs.rearrange("p (a b) -> p a b", a=4), w=["lamv"])
    ld(posf_i, pos_full, w=["posfi"])
    ld(poso_i, pos_own, w=["posoi"])
    ld(final_g_t, final_g_b, w=["fing"])

    OP("vector", lambda e: e.memset(ones_bf, 1.0), (), ["ones_bf"])
    OP("vector", lambda e: e.memset(one_f, 1.0), (), ["one_f"])
    OP("vector", lambda e: e.memset(ones_row, 1.0), (), ["ones_row"])

    OP("vector", lambda e: e.memset(eps_t, EPS), (), ["eps_t"])

    def rsqrt(dst, src, scale, reads, wtag):
        OP("scalar", lambda e: e.activation(out=dst, in_=src, func=AF.Sqrt, bias=eps_t[:, 0:1], scale=float(scale)),
           list(reads) + ["eps_t"], [wtag])
        OP("vector", lambda e: e.reciprocal(out=dst, in_=dst), [wtag], [wtag])

    def lam_block():
        OP("vector", lambda e: e.tensor_tensor(out=lam_tmp[:, 0, :], in0=lamv_t[:, 0, :], in1=lamv_t[:, 1, :], op=ALU.mult),
           ["lamv"], ["lamtmp0"])
        OP("vector", lambda e: e.tensor_tensor(out=lam_tmp[:, 1, :], in0=lamv_t[:, 2, :], in1=lamv_t[:, 3, :], op=ALU.mult),
           ["lamv"], ["lamtmp1"])
        OP("vector", lambda e: e.reduce_sum(out=lam_s[:, 0:2], in_=lam_tmp, axis=AX.X), ["lamtmp0", "lamtmp1"], ["lams01"])
        OP("scalar", lambda e: e.activation(out=lam_s[:, 4:6], in_=lam_s[:, 0:2], func=AF.Exp), ["lams01"], ["lams45"])
        OP("vector", lambda e: e.tensor_tensor(out=lam_s[:, 2:3], in0=lam_s[:, 4:5], in1=lam_s[:, 5:6], op=ALU.subtract),
           ["lams45"], ["lams2"])
        OP("vector", lambda e: e.tensor_scalar(out=lam_s[:, 3:4], in0=lam_s[:, 2:3], scalar1=float(LAM_INIT), scalar2=-1.0,
                                                op0=ALU.add, op1=ALU.mult), ["lams2"], ["neglam"])

    TWO_PI = 2.0 * math.pi

    def rope_tables(pos_i, pos_f, ang, cos_t, sin_t, nb, tag):
        OP("vector", lambda e: e.tensor_copy(out=pos_f, in_=pos_i), ["pos%si" % tag], ["pos%sf" % tag])
        OP("vector", lambda e: e.tensor_tensor(out=ang, in0=pos_f.unsqueeze(2).to_broadcast([128, nb, 8]),
                                                in1=invf_t.unsqueeze(1).to_broadcast([128, nb, 8]), op=ALU.mult),
           ["pos%sf" % tag, "invf"], ["ang" + tag])
        ki = ang_i[:, 0:nb, :]
        kf = ang_k[:, 0:nb, :]
        ng = ang_n[:, 0:nb, :]
        for dst, off, nm in ((sin_t, 0.5, "sin"), (cos_t, 0.75, "cos")):
            tg_ = nm + tag
            OP("vector", (lambda dst=dst, off=off: lambda e: e.tensor_scalar(out=dst, in0=ang, scalar1=1.0 / TWO_PI, scalar2=off,
                                                    op0=ALU.mult, op1=ALU.add))(), ["ang" + tag], [tg_])
            OP("vector", (lambda dst=dst: lambda e: e.tensor_copy(out=ki, in_=dst))(), [tg_], ["ang_i"])
            OP("vector", lambda e: e.tensor_copy(out=kf, in_=ki), ["ang_i"], ["ang_k"])
            OP("vector", (lambda dst=dst: lambda e: e.tensor_tensor(out=dst, in0=dst, in1=kf, op=ALU.subtract))(),
               [tg_, "ang_k"], [tg_])
            OP("vector", (lambda dst=dst: lambda e: e.tensor_scalar(out=ng, in0=dst, scalar1=0.0, scalar2=None, op0=ALU.is_lt))(),
               [tg_], ["ang_n"])
            OP("vector", (lambda dst=dst: lambda e: e.tensor_tensor(out=dst, in0=dst, in1=ng, op=ALU.add))(),
               [tg_, "ang_n"], [tg_])
            OP("vector", (lambda dst=dst: lambda e: e.tensor_scalar(out=dst, in0=dst, scalar1=TWO_PI, scalar2=-math.pi,
                                                    op0=ALU.mult, op1=ALU.add))(), [tg_], [tg_])
            OP("vector", (lambda dst=dst: lambda e: e.tensor_scalar(out=dst, in0=dst, scalar1=3.1415925, scalar2=-3.1415925,
                                                    op0=ALU.min, op1=ALU.max))(), [tg_], [tg_])
            OP("scalar", (lambda dst=dst: lambda e: e.activation(out=dst, in_=dst, func=AF.Sin))(), [tg_], [tg_])

    def const_block():
        lam_block()
        rope_tables(posf_i, posf_f, ang_f, cos_f, sin_f, NB, "f")
        rope_tables(poso_i, poso_f, ang_o, cos_o, sin_o, NM, "o")

    if stop == "c":
        const_block()

    if stop == "c":
        if "cs" in dbg:
            ld(dbg["cs"][:, 0:256], cos_f.rearrange("p a b -> p (a b)"), r=["cosf"])
            ld(dbg["cs"][:, 256:512], sin_f.rearrange("p a b -> p (a b)"), r=["sinf"])
            ld(dbg["mod"][:, 32:33], lam_s[:, 3:4], r=["neglam"])
        return nc, stack, Sd, dbg
    Wkv = A.buf(KB(30), [KC, 2048], BF16)
    hT = [A.buf(KB(152), [KC, TG], BF16), A.buf(KB(160), [KC, TG], BF16)]
    ktok = [A.buf(KB(168), [1024], BF16), A.buf(KB(170), [1024], BF16)]
    KTst = [A.buf(KB(172), [H, TG], BF16), A.buf(KB(176), [H, TG], BF16)]
    Vst = [A.buf(KB(180), [2, H * VP], BF16), A.buf(KB(185), [2, H * VP], BF16)]
    rt_p = [A.buf(KB(190) + i * 512, [16, 8], F32) for i in range(4)]
    w_in_v = w_in.rearrange("(j p) n -> p j n", p=128)

    FL0 = os.environ.get("P_FLAGS", "")
    for t4 in range(4) if "nowkv" not in FL0 else ():
        cs = slice(1024 + t4 * 512, 1024 + (t4 + 1) * 512)
        DMA("gpsimd", (lambda t4=t4, cs=cs: lambda e: e.dma_start(out=Wkv[:, :, t4 * 512:(t4 + 1) * 512],
                                                                   in_=w_in_v[:, :, cs]))(), (), ["Wkv"])

    def mod_group(g):
        s = g % 2
        cols = slice(g * 512, (g + 1) * 512)
        if g >= 2:
            ld(wada[s], w_ada[g * 128:(g + 1) * 128, :].rearrange("p (j n) -> p j n", j=KC), r=(), w=["wada%d" % s])
            ld(brow[s][0:1, :], b_ada[0:1, cols], w=["brow%d" % s])
        for j in range(KC):
            OP("tensor", (lambda j=j: lambda e: e.matmul(bank(0)[0:1, :], cact_t[:, j:j + 1], wada[s][:, j, :],
                                                          start=(j == 0), stop=(j == KC - 1)))(),
               ["cact", "wada%d" % s], ["psM"], sig=(j == KC - 1))
        OP("vector", lambda e: e.tensor_tensor(out=mrow[s][0:1, :], in0=bank(0)[0:1, :], in1=brow[s][0:1, :], op=ALU.add),
           ["psM", "brow%d" % s], ["mrow%d" % s])
        if g < 8:
            for i in range(4):
                OP("tensor", (lambda i=i: lambda e: e.matmul(bank(1)[:, g * 4 + i:g * 4 + i + 1],
                                                              mrow[s][0:1, i * 128:(i + 1) * 128], one_f[0:1, 0:1],
                                                              start=True, stop=True))(),
                   ["mrow%d" % s, "one_f"], ["pscol"], sig=(i == 3))
        else:
            OP("tensor", lambda e: e.matmul(bank(2), ones_row[0:1, :], mrow[s][0:1, :], start=True, stop=True),
               ["mrow%d" % s, "ones_row"], ["psG"])
            gc = slice((g - 8) * 512, (g - 7) * 512)
            OP("scalar", lambda e: e.activation(out=gate_b[:, gc], in_=bank(2), func=AF.Copy), ["psG"], ["gate_b"])

    for g in range(8):
        mod_group(g)
        if g == 1:
            const_block()
    OP("vector", lambda e: e.tensor_copy(out=shift_col, in_=bank(1)[:, 0:16]), ["pscol"], ["shift_col"])
    OP("vector", lambda e: e.scalar_tensor_tensor(out=ga_col, in0=bank(1)[:, 16:32], scalar=1.0, in1=normg_t,
                                                   op0=ALU.add, op1=ALU.mult), ["pscol", "normg"], ["ga_col"])
    if "mod" in dbg:
        ld(dbg["mod"][:, 0:16], shift_col, r=["shift_col"])
        ld(dbg["mod"][:, 16:32], ga_col, r=["ga_col"])
        ld(dbg["mod"][:, 32:33], lam_s[:, 3:4], r=["neglam"])
        ld(dbg["gate"], gate_b, r=["gate_b"])
        ld(dbg["cs"][:, 0:256], cos_f.rearrange("p a b -> p (a b)"), r=["cosf"])
        ld(dbg["cs"][:, 256:512], sin_f.rearrange("p a b -> p (a b)"), r=["sinf"])
    xs0_pre = A.buf(KB(110), [KC, TG], F32)
    ld(xs0_pre, xT[0:128, :].rearrange("p (j t) -> p j t", j=KC), r=(), w=["xs0", "wada0"])
    Sd.fence()
    if stop == "m":
        return nc, stack, Sd, dbg

    xs = [A.buf(KB(110), [KC, TG], F32), A.buf(KB(126), [KC, TG], F32)]
    sq = A.buf(KB(142), [KC, TG], BF16)
    rstd = [A.buf(KB(150), [TG], F32), A.buf(KB(151), [TG], F32)]

    def h_front(src, s, tag):
        h_front_a(src, s)
        h_front_b(s)

    def h_front_a(src, s):
        h_ld(src, s)
        h_sq(s)

    def h_ld(src, s):
        ld(xs[s], src, w=["xs%d" % s])

    def h_sq(s):
        OP("scalar", lambda e: e.activation(out=sq, in_=xs[s], func=AF.Square), ["xs%d" % s], ["sq"])

    def h_front_b(s):
        for j in range(KC):
            OP("tensor", (lambda j=j: lambda e: e.matmul(bank(0)[:, 0:TG], ones_bf, sq[:, j, :],
                                                          start=(j == 0), stop=(j == KC - 1)))(),
               ["sq", "ones_bf"], ["ps_ss"], sig=(j == KC - 1))

    def h_back(s, dst, dst_tag, split=False):
        h_back_1(s)
        h_back_2(s, dst, dst_tag, split)

    def h_back_1(s):
        rsqrt(rstd[s], bank(0)[:, 0:TG], 1.0 / D, ["ps_ss"], "rstd%d" % s)
        OP("vector", lambda e: e.tensor_tensor(out=xs[s], in0=xs[s], in1=rstd[s].unsqueeze(1).to_broadcast([128, KC, TG]),
                                                op=ALU.mult), ["xs%d" % s, "rstd%d" % s], ["xs%d" % s])

    def h_back_2(s, dst, dst_tag, split=False):
        act_js = list(range(0, KC, 2)) if split else list(range(KC))
        for j in act_js:
            OP("scalar", (lambda j=j: lambda e: e.activation(out=dst[:, j, :], in_=xs[s][:, j, :], func=AF.Identity,
                                                              bias=shift_col[:, j:j + 1], scale=ga_col[:, j:j + 1]))(),
               ["xs%d" % s, "ga_col", "shift_col"], [dst_tag], sig=(j == act_js[-1]))
        for j in (range(1, KC, 2) if split else ()):
            OP("vector", (lambda j=j: lambda e: e.tensor_scalar(out=dst[:, j, :], in0=xs[s][:, j, :],
                                                                 scalar1=ga_col[:, j:j + 1], scalar2=shift_col[:, j:j + 1],
                                                                 op0=ALU.mult, op1=ALU.add))(),
               ["xs%d" % s, "ga_col", "shift_col"], [dst_tag], sig=(j == KC - 1))

    for s in range(2) if "novmem" not in FL0 else ():
        OP("vector", (lambda s=s: lambda e: e.memset(Vst[s], 1.0))(), (), ["Vst%d" % s])

    def xT_grp(g):
        return xT[g * 128:(g + 1) * 128, :].rearrange("p (j t) -> p j t", j=KC)

    NG = int(os.environ.get('P_NG', S // TG))

    def rope(ps_k, nh, cos_t, sin_t, dst, src_tag, dst_tag, rt=None):
        rt = rt or rt_p
        kv = ps_k.rearrange("p (h c d) -> p (h c) d", h=nh, c=2)
        dv = dst.rearrange("p (h c d) -> p (h c) d", h=nh, c=2)
        n2 = nh * 2
        cb = cos_t.unsqueeze(1).to_broadcast([128, n2, 8])
        sb = sin_t.unsqueeze(1).to_broadcast([128, n2, 8])
        t1, t2 = kv[:, :, 0:8], kv[:, :, 8:16]
        r0, r1, r2, r3 = [r[:, 0:n2, :] for r in rt]
        OP("vector", lambda e: e.tensor_tensor(out=r0, in0=t1, in1=cb, op=ALU.mult), [src_tag], ["rt0"])
        OP("vector", lambda e: e.tensor_tensor(out=r1, in0=t2, in1=sb, op=ALU.mult), [src_tag], ["rt1"])
        OP("vector", lambda e: e.tensor_tensor(out=r2, in0=t2, in1=cb, op=ALU.mult), [src_tag], ["rt2"])
        OP("vector", lambda e: e.tensor_tensor(out=r3, in0=t1, in1=sb, op=ALU.mult), [src_tag], ["rt3"])
        OP("vector", lambda e: e.tensor_tensor(out=dv[:, :, 0:8], in0=r0, in1=r1, op=ALU.subtract),
           ["rt0", "rt1"], [dst_tag])
        OP("vector", lambda e: e.tensor_tensor(out=dv[:, :, 8:16], in0=r2, in1=r3, op=ALU.add),
           ["rt2", "rt3"], [dst_tag])
        OP("scalar", lambda e: e.activation(out=dv[:, :, 16:64], in_=kv[:, :, 16:64], func=AF.Copy),
           [src_tag], [dst_tag])

    FL = os.environ.get("P_FLAGS", "")

    def p_kv(tg, blk):
        s = tg % 2
        gb = tg * 2 + blk
        if "nokv" in FL:
            return
        for t4 in range(4):
            for j in range(KC):
                OP("tensor", (lambda t4=t4, j=j: lambda e: e.matmul(bank(1 + t4), hT[s][:, j, blk * 128:(blk + 1) * 128],
                                                                     Wkv[:, j, t4 * 512:(t4 + 1) * 512],
                                                                     start=(j == 0), stop=(j == KC - 1)))(),
                   ["hT%d" % s, "Wkv"], ["ps_k%d" % t4 if t4 < 2 else "ps_kv%d" % t4], sig=(j == KC - 1))
        for t4 in (2, 3) if "novev" not in FL else ():
            hh = (t4 - 2) * 4
            dstv = Vst[s][:, blk, hh * VP:(hh + 4) * VP].rearrange("p (h d) -> p h d", h=4)[:, :, 0:128]
            OP("scalar", (lambda t4=t4, dstv=dstv: lambda e: e.activation(
                out=dstv, in_=bank(t4 + 1).rearrange("p (h d) -> p h d", h=4), func=AF.Copy))(),
               ["ps_kv%d" % t4], ["Vst%d" % s])
        kb = gb % 2
        for hb in range(2) if "norope" not in FL else ():
            rope(bank(1 + hb), 4, cos_f[:, gb, :], sin_f[:, gb, :], ktok[kb][:, hb * 512:(hb + 1) * 512],
                 "ps_k%d" % hb, "ktok%d" % kb)

    def p_tr(tg, blk):
        s = tg % 2
        if "notr" in FL or "nokv" in FL:
            return
        gb = tg * 2 + blk
        kb = gb % 2
        for h in range(H):
            OP("tensor", (lambda h=h: lambda e: e.transpose(bank_bf(5)[:, h * 128:(h + 1) * 128],
                                                             ktok[kb][:, h * 128:(h + 1) * 128], ident))(),
               ["ktok%d" % kb, "ident"], ["ps_tr"], sig=(h == H - 1))
        OP("vector", lambda e: e.tensor_copy(out=KTst[s][:, :, blk * 128:(blk + 1) * 128],
                                              in_=bank_bf(5).rearrange("p (h t) -> p h t", h=H)),
           ["ps_tr"], ["KTst%d" % s])
        if blk == 1 and "nost" not in FL:
            ld(KT_d[:, :, tg * TG:(tg + 1) * TG].rearrange("h p t -> p h t"), KTst[s], q="gpsimd", r=["KTst%d" % s], w=["KT_d"])
            ld(V_d[tg * 2:tg * 2 + 2].rearrange("b p n -> p b n"), Vst[s], q="gpsimd", r=["Vst%d" % s], w=["V_d"])

    wg = [A.buf(KB(94), [KC, 128], F32), A.buf(KB(102), [KC, 128], F32)]
    gacc = [A.buf(KB(192), [128], F32), A.buf(KB(192.5), [128], F32)]
    bg = [A.buf(KB(193), [128], F32), A.buf(KB(193.5), [128], F32)]

    def gate_load(i):
        s_ = i % 2
        ld(wg[s_], w_gate[i * 128:(i + 1) * 128, :].rearrange("p (j n) -> p j n", j=KC), w=["wg%d" % s_])
        ld(bg[s_][0:1, :], b_ada[0:1, 2 * D + i * 128:2 * D + (i + 1) * 128], w=["bg%d" % s_])

    def gate_tile(i):
        s_ = i % 2
        OP("vector", lambda e: e.tensor_scalar(out=gacc[s_], in0=wg[s_][:, 0, :], scalar1=cact_t[:, 0:1], scalar2=None,
                                                op0=ALU.mult), ["wg%d" % s_, "cact"], ["gacc%d" % s_])
        for j in range(1, KC):
            OP("vector", (lambda j=j: lambda e: e.scalar_tensor_tensor(out=gacc[s_], in0=wg[s_][:, j, :],
                                                                        scalar=cact_t[:, j:j + 1], in1=gacc[s_],
                                                                        op0=ALU.mult, op1=ALU.add))(),
               ["wg%d" % s_, "cact", "gacc%d" % s_], ["gacc%d" % s_])
        OP("vector", lambda e: e.tensor_tensor(out=gacc[s_][0:1, :], in0=gacc[s_][0:1, :],
                                                in1=bg[s_][0:1, :], op=ALU.add),
           ["gacc%d" % s_, "bg%d" % s_], ["gacc%d" % s_])

    def gate_reduce(i):
        s_ = i % 2
        OP("tensor", lambda e: e.matmul(bank(6)[:, 0:128], ones_row, gacc[s_], start=True, stop=True),
           ["gacc%d" % s_, "ones_row"], ["ps_gate"])
        OP("scalar", lambda e: e.activation(out=gate_b[:, i * 128:(i + 1) * 128], in_=bank(6)[:, 0:128], func=AF.Copy),
           ["ps_gate"], ["gate_b"])

    h_sq(0)
    h_front_b(0)
    h_back(0, hT[0], "hT0")
    pending_tr = None
    for tg in range(NG):
        nxt = tg + 1 < NG
        if nxt:
            h_front_a(xT_grp(tg + 1), (tg + 1) % 2)
        p_kv(tg, 0)
        if nxt:
            h_front_b((tg + 1) % 2)
        if pending_tr is not None:
            p_tr(*pending_tr)
        pending_tr = (tg, 0)
        if nxt:
            h_back((tg + 1) % 2, hT[(tg + 1) % 2], "hT%d" % ((tg + 1) % 2))
        p_kv(tg, 1)
        p_tr(*pending_tr)
        pending_tr = (tg, 1)
        if tg == 0:
            gate_load(0)
        if tg + 1 < NG:
            gate_load(tg + 1)
        if tg > 0:
            gate_reduce(tg - 1)
        gate_tile(tg)
    p_tr(*pending_tr)
    gate_reduce(NG - 1)
    if "KT" in dbg:
        Sd.fence()
        ld(dbg["KT"], KT_d, r=["KT_d"])
        ld(dbg["V"], V_d, r=["V_d"])
    Sd.fence()
    if stop == "p":
        return nc, stack, Sd, dbg

    hTo = A.buf(KB(160), [KC, TE], BF16)
    def xoT_grp(g):
        return xoT[g * 128:(g + 1) * 128, :].rearrange("p (j t) -> p j t", j=KC)

    NGO = TE // TG
    Wpre = [A.buf(KB(62), [KC, 512], BF16), A.buf(KB(78), [KC, 512], BF16)]
    for i_ in range(2):
        DMA("gpsimd", (lambda i_=i_: lambda e: e.dma_start(out=Wpre[i_], in_=w_in_v[:, :, i_ * 512:(i_ + 1) * 512]))(),
            (), ["Wpre%d" % i_])
    xs.append(A.buf(KB(94), [KC, TG], F32))
    rstd.append(A.buf(KB(152), [TG], F32))
    for g in range(min(3, NGO)):
        h_ld(xoT_grp(g), g % 3)
    h_sq(0)
    h_front_b(0)
    h_back_1(0)
    for g in range(NGO):
        if g + 1 < NGO:
            h_sq((g + 1) % 3)
            h_front_b((g + 1) % 3)
            h_back_1((g + 1) % 3)
        h_back_2(g % 3, hTo[:, :, g * TG:(g + 1) * TG], "hTo", split=True)
        if g + 3 < NGO:
            h_ld(xoT_grp(g + 3), g % 3)
    Sd.fence()
    if stop == "oh":
        return nc, stack, Sd, dbg

    Wst = [A.buf(KB(98 + 16 * i), [KC, 512], BF16) for i in range(3)]
    qtok = [A.buf(KB(146), [512], BF16), A.buf(KB(147), [512], BF16)]
    stmp = [A.buf(KB(148), [512], F32), A.buf(KB(150), [512], F32)]
    rt_o = [A.buf(KB(152) + i * 512, [16, 8], F32) for i in range(4)]
    yext = A.buf(KB(78), [8, TE], BF16)

    wstate = {"n": 0}

    def load_w(col0):
        slot = wstate["n"] % 3
        wstate["n"] += 1
        DMA("gpsimd", lambda e: e.dma_start(out=Wst[slot], in_=w_in_v[:, :, col0:col0 + 512]), (), ["Wst%d" % slot])
        return slot

    tile_cols = [3072, 3584, 5120, 4096, 5632, 4608, 6144, 6656]
    slots = {}

    def want(i):
        if i < len(tile_cols) and i not in slots:
            slots[i] = load_w(tile_cols[i])

    def own(j, m):
        return hTo[:, j, m * EXT + 32:m * EXT + 160]

    def q_tr(cg, m):
        for i in range(4):
            OP("tensor", (lambda i=i: lambda e: e.transpose(bank_bf(5)[:, i * 128:(i + 1) * 128],
                                                             qtok[m % 2][:, i * 128:(i + 1) * 128], ident))(),
               ["qtok%d" % (m % 2), "ident"], ["bk5"], sig=(i == 3))
        OP("vector", lambda e: e.tensor_copy(out=QT[:, cg * 4:(cg + 1) * 4, m * 128:(m + 1) * 128],
                                              in_=bank_bf(5)[:, 0:512].rearrange("p (h t) -> p h t", h=4)),
           ["bk5"], ["QT"])

    want(0)
    want(1)
    for cg in range(2):
        pend = None
        for m in range(NM):
            pb = 1 + (m % 2)
            for j in range(KC):
                OP("tensor", (lambda j=j, m=m, pb=pb, cg=cg: lambda e: e.matmul(bank(pb), own(j, m),
                                                                          Wpre[cg][:, j, :], start=(j == 0), stop=(j == KC - 1)))(),
                   ["hTo", "Wpre%d" % cg], ["bk%d" % pb], sig=(j == KC - 1))
            rope(bank(pb), 4, cos_o[:, m, :], sin_o[:, m, :], qtok[m % 2], "bk%d" % pb, "qtok%d" % (m % 2), rt=rt_o)
            if pend is not None:
                q_tr(*pend)
            pend = (cg, m)
        q_tr(*pend)
    for cg in range(2):
        ti = cg
        want(ti + 2)
        sl = slots[ti]
        for m in range(NM):
            pb = 3 + (m % 2)
            for j in range(KC):
                OP("tensor", (lambda j=j, m=m, pb=pb, sl=sl: lambda e: e.matmul(bank(pb), own(j, m),
                                                                          Wst[sl][:, j, :], start=(j == 0), stop=(j == KC - 1)))(),
                   ["hTo", "Wst%d" % sl], ["bk%d" % pb], sig=(j == KC - 1))
            OP("scalar", (lambda m=m, pb=pb: lambda e: e.activation(out=stmp[m % 2], in_=bank(pb), func=AF.Silu))(),
               ["bk%d" % pb], ["stmp%d" % (m % 2)])
            OP("vector", (lambda m=m, cg=cg: lambda e: e.scalar_tensor_tensor(
                out=G[:, m, cg * 512:(cg + 1) * 512].rearrange("p (h d) -> p h d", h=4),
                in0=stmp[m % 2].rearrange("p (h d) -> p h d", h=4), scalar=float(1.0 - LAM_INIT),
                in1=subg_t.unsqueeze(1).to_broadcast([128, 4, 128]), op0=ALU.mult, op1=ALU.mult))(),
               ["stmp%d" % (m % 2), "subg"], ["G"])
    if "QT" in dbg:
        ld(dbg["QT"], QT.rearrange("p h t -> p (h t)"), r=["QT"])
        ld(dbg["G"], G.rearrange("p m f -> p (m f)"), r=["G"])

    pieces = [(0, 512), (512, 512), (1024, 256)]
    kctr = 0
    for t in range(2):
        for which in range(2):
            ti = 2 + 2 * t + which
            want(ti + 2)
            sl = slots[ti]
            for c4 in range(4):
                cc = 4 * t + c4
                for (p0, n) in pieces:
                    bk = 1 + (kctr % 4)
                    for j in range(KC):
                        OP("tensor", (lambda j=j, sl=sl, bk=bk, p0=p0, n=n, c4=c4: lambda e: e.matmul(
                            bank(bk)[:, 0:n], Wst[sl][:, j, c4 * 128:(c4 + 1) * 128], hTo[:, j, p0:p0 + n],
                            start=(j == 0), stop=(j == KC - 1)))(),
                           ["hTo", "Wst%d" % sl], ["bk%d" % bk], sig=(j == KC - 1))
                    if which == 0:
                        OP("scalar", (lambda bk=bk, n=n, p0=p0, cc=cc: lambda e: e.activation(
                            out=yext[:, cc, p0:p0 + n], in_=bank(bk)[:, 0:n], func=AF.Sigmoid))(),
                           ["bk%d" % bk], ["yext%d" % cc])
                    else:
                        OP("vector", (lambda bk=bk, n=n, p0=p0, cc=cc: lambda e: e.tensor_tensor(
                            out=yext[:, cc, p0:p0 + n], in0=bank(bk)[:, 0:n], in1=yext[:, cc, p0:p0 + n], op=ALU.mult))(),
                           ["bk%d" % bk, "yext%d" % cc], ["yext%d" % cc])
                    kctr += 1
    OP("vector", lambda e: e.tensor_tensor(out=yext[:, :, 0:32], in0=yext[:, :, 0:32],
                                            in1=ymask_t.unsqueeze(1).to_broadcast([128, 8, 32]), op=ALU.mult),
       ["yext%d" % cc_ for cc_ in range(8)] + ["ymask"], ["yext%d" % cc_ for cc_ in range(8)])
    for t in range(2):
        ti = 6 + t
        want(ti + 2)
        sl = slots[ti]
        for c4 in range(4):
            cc = 4 * t + c4
            for hf in range(2):
                bk = 5 + hf
                for j in range(KC):
                    OP("tensor", (lambda j=j, sl=sl, c4=c4, hf=hf, bk=bk: lambda e: e.matmul(
                        bank(bk), Wst[sl][:, j, c4 * 128:(c4 + 1) * 128],
                        hTo[:, j, :].rearrange("p (m t) -> p m t", t=EXT)[:, 4 * hf:4 * hf + 4, 32:160],
                        start=(j == 0), stop=(j == KC - 1)))(),
                       ["hTo", "Wst%d" % sl], ["bk%d" % bk], sig=(j == KC - 1))
                OP("scalar", (lambda cc=cc, hf=hf, bk=bk: lambda e: e.activation(
                    out=ycT[:, cc, hf * 512:(hf + 1) * 512], in_=bank(bk), func=AF.Silu))(),
                   ["bk%d" % bk], ["ycT"])
    Sd.fence()
    if stop == "cp":
        return nc, stack, Sd, dbg

    sw = A.buf(KB(98), [8, 512], BF16)
    vf = A.buf(KB(106), [8, T], F32)
    vb = A.buf(KB(138), [8, 512], BF16)
    vsq = A.buf(KB(146), [8, 512], BF16)
    st = [A.buf(KB(154) + i * 2048, [512], F32) for i in range(4)]
    diag = [A.buf(KB(162), [CW, 128], BF16), A.buf(KB(170), [CW, 128], BF16)]
    Wpw = A.buf(KB(178), [8, 1024], BF16)
    DMA("gpsimd", lambda e: e.dma_start(out=Wpw, in_=w_pw.rearrange("(j p) n -> p j n", p=128)), (), ["Wpw"])
    yext_v = [yext[:, cc, :].rearrange("p (m t) -> p m t", t=EXT) for cc in range(8)]
    dctr = {"n": 0, "b": 0}

    def diag_build(cc):
        k = dctr["b"]
        dctr["b"] += 1
        dg = diag[k % 2]
        dtag = "diag%d" % (k % 2)
        for j in range(CW):
            OP("vector", (lambda j=j, cc=cc, dg=dg: lambda e: e.tensor_scalar(
                out=dg[:, j, :], in0=ident, scalar1=dww_t[:, cc, j:j + 1], scalar2=None, op0=ALU.mult))(),
               ["ident", "dww"], [dtag], sig=(j == CW - 1))

    def conv_cc(cc, hf):
        dg = diag[dctr["n"] % 2]
        dtag = "diag%d" % (dctr["n"] % 2)
        dctr["n"] += 1
        bk = 1 + (dctr["n"] % 2)
        for j in range(CW):
            OP("tensor", (lambda j=j, cc=cc, hf=hf, bk=bk, dg=dg: lambda e: e.matmul(
                bank(bk), dg[:, j, :], yext_v[cc][:, 4 * hf:4 * hf + 4, 2 + j:2 + j + 128],
                start=(j == 0), stop=(j == CW - 1)))(),
               [dtag, "yext%d" % cc], ["ps_cv%d" % bk], sig=(j == CW - 1))
        OP("scalar", (lambda cc=cc, hf=hf, bk=bk: lambda e: e.activation(
            out=vf[:, cc, hf * 512:(hf + 1) * 512], in_=bank(bk), func=AF.Identity, bias=dwb_t[:, cc:cc + 1], scale=1.0))(),
           ["ps_cv%d" % bk, "dwb"], ["vf%d" % hf])

    def chain_front(hf):
        vh_ = vf[:, :, hf * 512:(hf + 1) * 512]
        OP("vector", (lambda vh_=vh_: lambda e: e.tensor_copy(out=vb, in_=vh_))(), ["vf%d" % hf], ["vb"])
        OP("scalar", (lambda vh_=vh_: lambda e: e.activation(out=vsq, in_=vh_, func=AF.Square))(), ["vf%d" % hf], ["vsq"])

    def chain_stats(hf):
        for cc in range(8):
            OP("tensor", (lambda cc=cc: lambda e: e.matmul(bank(3), ones_bf, vb[:, cc, :], start=(cc == 0), stop=(cc == 7)))(),
               ["vb", "ones_bf"], ["ps_s1"], sig=(cc == 7))
        for cc in range(8):
            OP("tensor", (lambda cc=cc: lambda e: e.matmul(bank(4), ones_bf, vsq[:, cc, :], start=(cc == 0), stop=(cc == 7)))(),
               ["vsq", "ones_bf"], ["ps_s2"], sig=(cc == 7))

    def chain_mid(hf):
        vh_ = vf[:, :, hf * 512:(hf + 1) * 512]
        OP("vector", lambda e: e.tensor_scalar(out=st[0], in0=bank(3), scalar1=1.0 / 1024, scalar2=None, op0=ALU.mult),
           ["ps_s1"], ["st0"])
        OP("vector", lambda e: e.tensor_tensor(out=st[1], in0=st[0], in1=st[0], op=ALU.mult), ["st0"], ["st1"])
        OP("vector", lambda e: e.scalar_tensor_tensor(out=st[2], in0=bank(4), scalar=1.0 / 1024, in1=st[1],
                                                       op0=ALU.mult, op1=ALU.subtract), ["ps_s2", "st1"], ["st2"])
        rsqrt(st[3], st[2], 1.0, ["st2"], "st3")
        OP("vector", (lambda vh_=vh_: lambda e: e.tensor_tensor(out=vh_, in0=vh_, in1=st[0].unsqueeze(1).to_broadcast([128, 8, 512]),
                                                                op=ALU.subtract))(), ["vf%d" % hf, "st0"], ["vf%d" % hf])
        OP("vector", (lambda vh_=vh_: lambda e: e.tensor_tensor(out=vh_, in0=vh_, in1=st[3].unsqueeze(1).to_broadcast([128, 8, 512]),
                                                                op=ALU.mult))(), ["vf%d" % hf, "st3"], ["vf%d" % hf])

    def chain_silu(hf):
        for cc in range(8):
            OP("scalar", (lambda cc=cc, hf=hf: lambda e: e.activation(
                out=sw[:, cc, :], in_=vf[:, cc, hf * 512:(hf + 1) * 512], func=AF.Silu,
                bias=lnb_t[:, cc:cc + 1], scale=lng_t[:, cc:cc + 1]))(),
               ["vf%d" % hf, "lng", "lnb"], ["sw"])

    def pw(hf):
        for co in range(8):
            bk = 5 + (co % 2)
            for ci in range(8):
                OP("tensor", (lambda co=co, ci=ci, bk=bk: lambda e: e.matmul(
                    bank(bk), Wpw[:, ci, co * 128:(co + 1) * 128], sw[:, ci, :], start=(ci == 0), stop=(ci == 7)))(),
                   ["Wpw", "sw"], ["ps_pw%d" % bk], sig=(ci == 7))
            OP("vector", (lambda co=co, hf=hf, bk=bk: lambda e: e.scalar_tensor_tensor(
                out=ycT[:, co, hf * 512:(hf + 1) * 512], in0=bank(bk), scalar=bpw_t[:, co:co + 1],
                in1=ycT[:, co, hf * 512:(hf + 1) * 512], op0=ALU.add, op1=ALU.mult))(),
               ["ps_pw%d" % bk, "bpw", "ycT"], ["ycT"])

    seq = [(cc, 0) for cc in range(8)] + [(cc, 1) for cc in range(8)]
    diag_build(seq[0][0])
    for i, (cc, hf) in enumerate(seq):
        conv_cc(cc, hf)
        if i + 1 < len(seq):
            diag_build(seq[i + 1][0])
        if hf == 1 and cc == 0:
            chain_front(0)
        if hf == 1 and cc == 1:
            chain_stats(0)
        if hf == 1 and cc == 2:
            chain_mid(0)
        if hf == 1 and cc == 4:
            chain_silu(0)
    chain_front(1)
    chain_stats(1)
    pw(0)
    chain_mid(1)
    chain_silu(1)
    kt0_pre = A.buf(KB(110), [S], BF16)
    vh0_pre = A.buf(KB(126), [NB, VP], BF16)
    ld(kt0_pre, KT_d[0], r=(), w=["kt0", "vf0", "vf1"])
    for q4 in range(4):
        ld(vh0_pre[:, q4 * 8:(q4 + 1) * 8, :],
           V_d[q4 * 8:(q4 + 1) * 8, :, 0:VP].rearrange("b p n -> p b n"), r=(), w=["vh0", "vf0", "vf1"])
    pw(1)
    Sd.fence()
    if "ycT" in dbg:
        ld(dbg["ycT"], ycT.rearrange("p c t -> p (c t)"), r=["ycT"])
        Sd.fence()
    if stop == "o":
        return nc, stack, Sd, dbg

    kt = [A.buf(KB(110), [S], BF16), A.buf(KB(118), [S], BF16)]
    vh = [A.buf(KB(126), [NB, VP], BF16), A.buf(KB(135), [NB, VP], BF16)]
    NSL = 3
    pT = [A.buf(KB(144 + 2 * i), [1024], BF16) for i in range(NSL)]
    dtmp = [A.buf(KB(150), [128], F32), A.buf(KB(150.5), [128], F32)]
    rec = [A.buf(KB(151), [8], F32), A.buf(KB(151) + 32, [8], F32)]
    ssq = A.buf(KB(151) + 64, [8], F32)
    rsv = A.buf(KB(151) + 96, [8], F32)
    dd_h = A.buf(KB(152), [NM, 128], F32)
    sqd = A.buf(KB(156), [NM, 128], F32)
    SC_BANK = [0, 2, 6]

    def acc_ap(am, c):
        return psum[:, (4 + am) * 512 + c * 256:(4 + am) * 512 + c * 256 + 130]

    NPRE = 10
    late_off = {10: 110, 11: 114, 12: 126, 13: 130, 14: 118, 15: 122}
    late_tag = {10: "kt0", 11: "kt0", 12: "vh0", 13: "vh0", 14: "kt1", 15: "kt1"}
    Wout = [A.buf(KB(160) + k * 4096, [D], BF16) if k < NPRE else A.buf(KB(late_off[k]), [D], BF16)
            for k in range(KC)]
    w_out_v = w_out.rearrange("(j p) n -> p j n", p=128)
    for k in range(NPRE):
        DMA("gpsimd", (lambda k=k: lambda e: e.dma_start(out=Wout[k], in_=w_out_v[:, k, :]))(), (), ["Wout%d" % k])

    QQ = A.buf(KB(78), [H, NM * 256], BF16)
    QQv = QQ.rearrange("p h (m c q) -> p (h m) c q", c=2, q=128)
    QTv = QT.rearrange("p h (m q) -> p (h m) q", q=128)
    OP("vector", lambda e: e.memset(QQv[0:64, :, 1, :], 0.0), (), ["QQa"])
    OP("scalar", lambda e: e.activation(out=QQv[64:128, :, 0, :], in_=QTv[64:128], func=AF.Copy, scale=0.0), ["QT"], ["QQb"])
    for h in range(H):
        for c in range(2):
            dst = QQ[c * 64:(c + 1) * 64, h, :].rearrange("p (m c q) -> p m c q", c=2, q=128)[:, :, c, :]
            src = QT[c * 64:(c + 1) * 64, h, :].rearrange("p (m q) -> p m q", q=128)
            if c == 0:
                OP("vector", (lambda dst=dst, src=src: lambda e: e.tensor_copy(out=dst, in_=src))(), ["QT", "QQa"], ["QQa"])
            else:
                OP("scalar", (lambda dst=dst, src=src: lambda e: e.activation(out=dst, in_=src, func=AF.Copy))(), ["QT", "QQb"], ["QQb"])
    state = {"pc": 0, "q": []}

    zeros_bf = A.buf(KB(151) + 128, [128], BF16)
    OP("vector", lambda e: e.memset(zeros_bf, 0.0), (), ["zeros_bf"])

    def flush_one():
        pi = state["q"].pop(0)
        if pi[2] == 0:
            am_, s__ = pi[6], pi[5]
            OP("tensor", lambda e: e.matmul(psum[:, (4 + am_) * 512:(4 + am_) * 512 + 386], zeros_bf, kt[s__][:, 0:386],
                                            start=True, stop=False),
               ["zeros_bf", "kt%d" % s__], ["ps_acc%d" % am_], sig=False)
        av(pi)
        if pi[2] == pi[3] - 1:
            post(pi[0], pi[1], pi[6])
            if pi[1] == NM - 1:
                epilogue_a(pi[0])
                state["epi"] = pi[0]

    def av(item):
        (h, m, t, nq, sl, s_, am) = item
        for e_ in range(4):
            blk = 4 * t + e_
            for c in range(2):
                OP("tensor", (lambda e_=e_, c=c, blk=blk, sl=sl, s_=s_, am=am, t=t, nq=nq: lambda e: e.matmul(
                    acc_ap(am, c), pT[sl][:, (e_ * 2 + c) * 128:(e_ * 2 + c + 1) * 128], vh[s_][:, blk, 0:130],
                    start=False, stop=(t == nq - 1 and e_ == 3 and c == 1)))(),
                   ["pT%d" % sl, "vh%d" % s_], ["ps_acc%d" % am], sig=(e_ == 3 and c == 1))

    def post(h, m, am):
        rc = rec[am]
        a0 = acc_ap(am, 0)
        a1 = acc_ap(am, 1)
        sums = psum[:, (4 + am) * 512:(5 + am) * 512].rearrange("p (c n) -> p c n", c=2)[:, :, 128]
        OP("vector", lambda e: e.reciprocal(out=rc[:, 0:2], in_=sums), ["ps_acc%d" % am], ["rec%d" % am])
        OP("vector", lambda e: e.tensor_tensor(out=rc[:, 2:3], in0=rc[:, 1:2], in1=lam_s[:, 3:4], op=ALU.mult),
           ["rec%d" % am, "neglam"], ["rec%d" % am])
        OP("vector", lambda e: e.tensor_scalar(out=dtmp[am], in0=a0[:, 0:128], scalar1=rc[:, 0:1], scalar2=None, op0=ALU.mult),
           ["ps_acc%d" % am, "rec%d" % am], ["dtmp%d" % am])
        OP("vector", lambda e: e.scalar_tensor_tensor(out=dd_h[:, m, :], in0=a1[:, 0:128], scalar=rc[:, 2:3], in1=dtmp[am],
                                                       op0=ALU.mult, op1=ALU.add),
           ["ps_acc%d" % am, "rec%d" % am, "dtmp%d" % am], ["dd_h"])

    def epilogue_a(h):
        OP("vector", lambda e: e.tensor_tensor(out=sqd, in0=dd_h, in1=dd_h, op=ALU.mult), ["dd_h"], ["sqd"])
        OP("vector", lambda e: e.reduce_sum(out=ssq[:, 0:8], in_=sqd, axis=AX.X), ["sqd"], ["ssq"])
        OP("vector", lambda e: e.tensor_copy(out=sqd, in_=dd_h), ["dd_h", "ssq"], ["sqd"])

    def epilogue_b(h):
        OP("scalar", lambda e: e.activation(out=rsv[:, 0:8], in_=ssq[:, 0:8], func=AF.Ln, bias=eps_t[:, 0:1], scale=1.0 / DV),
           ["ssq", "eps_t"], ["rsv"])
        OP("scalar", lambda e: e.activation(out=rsv[:, 0:8], in_=rsv[:, 0:8], func=AF.Exp, scale=-0.5), ["rsv"], ["rsv"])
        OP("vector", lambda e: e.tensor_tensor(out=sqd, in0=sqd, in1=rsv[:, 0:8].unsqueeze(2).to_broadcast([128, NM, 128]),
                                                op=ALU.mult), ["sqd", "rsv"], ["sqd"])
        OP("vector", lambda e: e.tensor_tensor(out=ya[:, :, h * 128:(h + 1) * 128], in0=sqd,
                                                in1=G[:, :, h * 128:(h + 1) * 128], op=ALU.mult), ["sqd", "G"], ["ya"])

    hm = 0
    for h in range(H):
        s_ = h % 2
        if h > 0:
            ld(kt[s_], KT_d[h], r=["KT_d"], w=["kt%d" % s_])
            for q4 in range(4):
                ld(vh[s_][:, q4 * 8:(q4 + 1) * 8, :],
                   V_d[q4 * 8:(q4 + 1) * 8, :, h * VP:(h + 1) * VP].rearrange("b p n -> p b n"), r=["V_d"], w=["vh%d" % s_])
        for m in range(NM):
            am = hm % 2
            nq = m + 1
            for t in range(nq):
                sl = state["pc"] % NSL
                state["pc"] += 1
                scp = psum[:, SC_BANK[sl] * 512:SC_BANK[sl] * 512 + 1024]
                for e_ in range(4):
                    blk = 4 * t + e_
                    OP("tensor", (lambda e_=e_, blk=blk, scp=scp, s_=s_, h=h, m=m: lambda e: e.matmul(
                        scp[:, e_ * 256:(e_ + 1) * 256],
                        kt[s_][:, blk * 128:(blk + 1) * 128],
                        QQ[:, h, m * 256:(m + 1) * 256], start=True, stop=True))(),
                       ["kt%d" % s_, "QQa", "QQb"], ["ps_sc%d" % sl], sig=(e_ == 3))
                OP("scalar", (lambda sl=sl, scp=scp: lambda e: e.activation(out=pT[sl], in_=scp, func=AF.Exp,
                                                                            scale=1.0 / math.sqrt(DQ)))(),
                   ["ps_sc%d" % sl], ["pT%d" % sl])
                if t == nq - 1:
                    OP("vector", (lambda sl=sl: lambda e: e.tensor_tensor(
                        out=pT[sl].rearrange("p (a c q) -> p a c q", a=4, c=2),
                        in0=pT[sl].rearrange("p (a c q) -> p a c q", a=4, c=2),
                        in1=masks.unsqueeze(2).to_broadcast([128, 4, 2, 128]), op=ALU.mult))(),
                       ["pT%d" % sl, "masks"], ["pT%d" % sl])
                state["q"].append((h, m, t, nq, sl, s_, am))
                if len(state["q"]) > NSL - 1:
                    flush_one()
            if h == H - 1 and m == 2:
                for k in (10, 11, 12, 13):
                    DMA("gpsimd", (lambda k=k: lambda e: e.dma_start(out=Wout[k], in_=w_out_v[:, k, :]))(), (),
                        ["Wout%d" % k, late_tag[k]])
            if state.get("epi") is not None and m == 2:
                epilogue_b(state["epi"])
                state["epi"] = None
            hm += 1
    while state["q"]:
        flush_one()
    if state.get("epi") is not None:
        epilogue_b(state["epi"])
    for m in range(NM):
        for h in range(H):
            OP("tensor", (lambda h=h, m=m: lambda e: e.transpose(bank_bf(0)[:, h * 128:(h + 1) * 128],
                                                                  ya[:, m, h * 128:(h + 1) * 128], ident))(),
               ["ya", "ident"], ["ps_tr", "ps_sc0"], sig=(h == H - 1))
        OP("vector", (lambda m=m: lambda e: e.tensor_copy(out=yaT[:, :, m * 128:(m + 1) * 128],
                                                           in_=bank_bf(0).rearrange("p (h t) -> p h t", h=H)))(),
           ["ps_tr"], ["yaT"])
    Sd.fence()
    if "yaT" in dbg:
        ld(dbg["yaT"], yaT.rearrange("p c t -> p (c t)"), r=["yaT"])
        Sd.fence()
    if stop == "a":
        return nc, stack, Sd, dbg

    for k in (14, 15):
        DMA("gpsimd", (lambda k=k: lambda e: e.dma_start(out=Wout[k], in_=w_out_v[:, k, :]))(), (), ["Wout%d" % k])
    xo_t = [A.buf(KB(78), [D], F32), A.buf(KB(86), [D], F32)]
    xnew = [A.buf(KB(134), [D], F32), A.buf(KB(142), [D], F32)]
    junkb = A.buf(KB(150), [D], BF16)
    fs = A.buf(KB(154), [8], F32)
    for m in range(NM):
        p_ = m % 2
        ld(xo_t[p_], xo[m * 128:(m + 1) * 128, :], w=["xo%d" % p_])
        for n4 in range(4):
            bk = 4 * p_ + n4
            cols = slice(n4 * 512, (n4 + 1) * 512)
            for k in range(KC):
                lhs = yaT[:, k, m * 128:(m + 1) * 128] if k < 8 else ycT[:, k - 8, m * 128:(m + 1) * 128]
                OP("tensor", (lambda k=k, lhs=lhs, bk=bk, cols=cols: lambda e: e.matmul(
                    bank(bk), lhs, Wout[k][:, cols], start=(k == 0), stop=(k == KC - 1)))(),
                   ["yaT", "ycT", "Wout%d" % k], ["ps_o%d" % bk], sig=(k == KC - 1))
            OP("vector", (lambda bk=bk, cols=cols, p_=p_: lambda e: e.tensor_tensor(
                out=xnew[p_][:, cols], in0=bank(bk), in1=gate_b[:, cols], op=ALU.mult))(),
               ["ps_o%d" % bk, "gate_b"], ["xnew%d" % p_])
            OP("vector", (lambda cols=cols, p_=p_: lambda e: e.tensor_tensor(
                out=xnew[p_][:, cols], in0=xnew[p_][:, cols], in1=xo_t[p_][:, cols], op=ALU.add))(),
               ["xnew%d" % p_, "xo%d" % p_], ["xnew%d" % p_])
        OP("vector", lambda e: e.memset(fs[:, 0:1], 0.0), (), ["fs"])
        OP("scalar", (lambda p_=p_: lambda e: e.activation(out=junkb, in_=xnew[p_], func=AF.Square, accum_out=fs[:, 0:1]))(),
           ["xnew%d" % p_, "fs"], ["fs", "junkb"])
        rsqrt(fs[:, 1:2], fs[:, 0:1], 1.0 / D, ["fs"], "fs1")
        OP("vector", (lambda p_=p_: lambda e: e.scalar_tensor_tensor(
            out=xnew[p_], in0=xnew[p_], scalar=fs[:, 1:2], in1=final_g_t, op0=ALU.mult, op1=ALU.mult))(),
           ["xnew%d" % p_, "fs1", "fing"], ["xnew%d" % p_])
        ld(out[m * 128:(m + 1) * 128, :], xnew[p_], q="gpsimd", r=["xnew%d" % p_], w=["out"])
    return nc, stack, Sd, dbg


def finish_program(nc, stack, Sd):
    Sd.fence(engines=("sync",))
    with nc.Block() as block:
        @block.tensor
        def _(e):
            Sd.emit("tensor", e)

        @block.vector
        def _(e):
            Sd.emit("vector", e)

        @block.scalar
        def _(e):
            Sd.emit("scalar", e)

        @block.gpsimd
        def _(e):
            Sd.emit("gpsimd", e)

        @block.sync
        def _(e):
            Sd.emit("sync", e)
    stack.close()
    return nc


def col_layout(v, nchunk):
    return np.ascontiguousarray(np.asarray(v, np.float32).reshape(nchunk, 128).T)


def make_in_maps(x, c, positions, norm_g, w_ada, b_ada, w_in, lambda_q1, lambda_k1, lambda_q2, lambda_k2,
                 subln_g, conv_dw_w, conv_dw_b, conv_ln_g, conv_ln_b, w_pw, b_pw, w_out, final_g):
    f32 = np.float32
    x = np.asarray(x, f32)
    c = np.asarray(c, f32)
    positions = np.asarray(positions, np.int32)
    w_ada_f = np.asarray(w_ada, f32)[0]
    w_ada0 = np.ascontiguousarray(
        w_ada_f[:, :2 * D].reshape(KC, 128, 8, 512).transpose(2, 1, 0, 3).reshape(8 * 128, KC * 512))
    w_gate0 = np.ascontiguousarray(
        w_ada_f[:, 2 * D:].reshape(KC, 128, 16, 128).transpose(2, 1, 0, 3).reshape(16 * 128, KC * 128))
    w_in0 = np.ascontiguousarray(np.asarray(w_in, f32)[0])
    w_pw0 = np.ascontiguousarray(np.asarray(w_pw, f32)[0])
    w_out0 = np.ascontiguousarray(np.asarray(w_out, f32)[0])
    b_ada0 = np.ascontiguousarray(np.asarray(b_ada, f32)[0].reshape(1, -1))
    invf = (THETA ** (-np.arange(0, 16, 2, dtype=np.float32) / 16.0)).astype(f32)
    invf_b = np.ascontiguousarray(np.broadcast_to(invf[None, :], (128, 8)))
    ident = np.eye(128, dtype=f32).astype(ml_dtypes.bfloat16)
    lam_vecs = np.concatenate([np.asarray(v, f32)[0] for v in (lambda_q1, lambda_k1, lambda_q2, lambda_k2)])
    lam_b = np.ascontiguousarray(np.broadcast_to(lam_vecs[None, :], (128, 4 * DQ)))
    dww = np.asarray(conv_dw_w, f32)[0]
    dww_col = np.ascontiguousarray(dww.T.reshape(8, 128, CW).transpose(1, 0, 2).reshape(128, 8 * CW))
    common = dict(
        invf=invf_b, normg_col=col_layout(np.asarray(norm_g, f32)[0], KC),
        final_g_b=np.ascontiguousarray(np.broadcast_to(np.asarray(final_g, f32)[None, :], (128, D))),
        subln_g_b=np.ascontiguousarray(np.broadcast_to(np.asarray(subln_g, f32)[0][None, :], (128, DV))),
        dww_col=dww_col, dwb_col=col_layout(np.asarray(conv_dw_b, f32)[0], 8),
        lng_col=col_layout(np.asarray(conv_ln_g, f32)[0], 8), lnb_col=col_layout(np.asarray(conv_ln_b, f32)[0], 8),
        bpw_col=col_layout(np.asarray(b_pw, f32)[0], 8), b_ada=b_ada0, lam_vecs=lam_b,
        w_ada=w_ada0, w_gate=w_gate0, w_in=w_in0, w_pw=w_pw0, w_out=w_out0, ident=ident,
    )
    def grp_layout(xt):
        ntok = xt.shape[1]
        return np.ascontiguousarray(
            xt.reshape(KC, 128, ntok // TG, TG).transpose(2, 1, 0, 3).reshape((ntok // TG) * 128, KC * TG))

    xT_b = [grp_layout(x[b].T) for b in range(2)]
    in_maps = []
    own_idx = []
    for core in range(8):
        b, r = core // 4, core % 4
        blocks = [4 * m + r for m in range(NM)]
        idx = np.concatenate([np.arange(128 * i, 128 * i + 128) for i in blocks])
        own_idx.append((b, idx))
        ext = np.concatenate([np.arange(128 * i - 32, 128 * i + 128) for i in blocks])
        valid = ext >= 0
        xe = np.zeros((TE, D), f32)
        xe[valid] = x[b][ext[valid]]
        ymask = np.ones((128, 32), f32)
        if r == 0:
            ymask[:] = 0.0
        mk = np.zeros((128, 4, 128), f32)
        kv = np.arange(128)[:, None]
        q = np.arange(128)[None, :]
        for j in range(4):
            if j < r:
                mk[:, j, :] = 1.0
            elif j == r:
                mk[:, j, :] = (q >= kv).astype(f32)
        m = dict(common)
        m.update(
            xT=xT_b[b], xoT=grp_layout(xe.T), xo=np.ascontiguousarray(x[b][idx]),
            c_col=col_layout(c[b], KC),
            pos_full=np.ascontiguousarray(positions[b].reshape(NB, 128).T),
            pos_own=np.ascontiguousarray(positions[b][idx].reshape(NM, 128).T),
            masks=mk.reshape(128, 512).astype(ml_dtypes.bfloat16), ymask=ymask,
        )
        in_maps.append(m)
    return in_maps, own_idx


def kernel(**inputs):
    in_maps, own_idx = make_in_maps(**inputs)
    nc, stack, Sd, _ = build_program()
    finish_program(nc, stack, Sd)
    res = run_bass_kernel_spmd(nc, in_maps, core_ids=list(range(8)))
    outp = np.zeros((2, S, D), np.float32)
    for core in range(8):
        b, idx = own_idx[core]
        outp[b, idx] = np.asarray(res.results[core]["out"], np.float32)
    return outp
```
